# Optimizing a Trainium2 kernel written in Bass

```python
import math
import jax, jax.numpy as jnp
from jax import lax
import numpy as np

D_MODEL = 2048
BATCH = 4
SEQ = 2048
DEPTH = 1

MEM_LEN = 256
EPS = 1e-6
DA_HEADS = 8
DA_HEAD_DIM = D_MODEL // 32
DA_V_DIM = 2 * DA_HEAD_DIM
DA_WIDTH = DA_HEADS * DA_V_DIM
Q_BLOCK = 128
SSD_HEAD_DIM = 64
SSD_WIDTH = D_MODEL
SSD_HEADS = SSD_WIDTH // SSD_HEAD_DIM
SSD_GROUPS = 4
SSD_STATE = 128
SSD_CONV = 4
SSD_CHUNK = 128
CONV_DIM = SSD_WIDTH + 2 * SSD_GROUPS * SSD_STATE
MIX_WIDTH = DA_WIDTH + SSD_WIDTH
SPLITS = (DA_WIDTH, 2 * DA_WIDTH, 3 * DA_WIDTH,
          3 * DA_WIDTH + SSD_WIDTH, 3 * DA_WIDTH + SSD_WIDTH + CONV_DIM)
IN_COLS = 3 * DA_WIDTH + SSD_WIDTH + CONV_DIM + SSD_HEADS
X_HEADS = 4
X_HEAD_DIM = D_MODEL // 16
X_WIDTH = X_HEADS * X_HEAD_DIM
D_FF = -(-8 * D_MODEL // (3 * 256)) * 256

kernel_name = "hybrid_diffattn_ssd_xattn_swiglu"


def rms_norm(x, w, eps=EPS):
    xf = x.astype(jnp.float32)
    y = xf * lax.rsqrt(jnp.mean(xf * xf, axis=-1, keepdims=True) + eps)
    return (y * w.astype(jnp.float32)).astype(x.dtype)


def diff_attention(q, k, v, q_norm_w, k_norm_w, lam, subln_w, lambda_init):
    b, s = q.shape[:2]
    q = rms_norm(q, q_norm_w) * (DA_HEAD_DIM ** -0.5)
    k = rms_norm(k, k_norm_w)
    nb = s // Q_BLOCK
    qb = q.reshape(b, nb, Q_BLOCK, DA_HEADS, 2, DA_HEAD_DIM).transpose(1, 0, 2, 3, 4, 5)
    k_pos = jnp.arange(s)

    def block(args):
        q_blk, i = args
        sc = jnp.einsum('bqhmd,bkhmd->bhmqk', q_blk, k,
                        preferred_element_type=jnp.float32)
        q_pos = i * Q_BLOCK + jnp.arange(Q_BLOCK)
        mask = k_pos[None, :] <= q_pos[:, None]
        p = jax.nn.softmax(jnp.where(mask, sc, -jnp.inf), axis=-1)
        p = p[:, :, 0] - lam * p[:, :, 1]
        return jnp.einsum('bhqk,bkhv->bqhv', p.astype(v.dtype), v)

    o = lax.map(block, (qb, jnp.arange(nb)))
    o = o.transpose(1, 0, 2, 3, 4).reshape(b, s, DA_HEADS, DA_V_DIM)
    o = rms_norm(o, subln_w) * (1.0 - lambda_init)
    return o.reshape(b, s, DA_WIDTH)


def ssd_mixer(z, xbc, dt_raw, conv_w, conv_b, dt_bias, a_log, d_skip, norm_w):
    b, s = z.shape[:2]
    G, E, P, N, L = SSD_GROUPS, SSD_HEADS // SSD_GROUPS, SSD_HEAD_DIM, SSD_STATE, SSD_CHUNK
    nc = s // L
    xbc = lax.conv_general_dilated(xbc, conv_w[:, None, :], window_strides=(1,),
                                   padding=[(SSD_CONV - 1, 0)],
                                   dimension_numbers=('NWC', 'WIO', 'NWC'),
                                   feature_group_count=CONV_DIM) + conv_b
    xbc = jax.nn.silu(xbc)
    xs, bm, cm = jnp.split(xbc, [SSD_WIDTH, SSD_WIDTH + G * N], axis=-1)
    dt = jax.nn.softplus(dt_raw.astype(jnp.float32) + dt_bias.astype(jnp.float32))
    a = -jnp.exp(a_log.astype(jnp.float32))
    x = xs.reshape(b, nc, L, G, E, P)
    bc = bm.reshape(b, nc, L, G, N)
    cc = cm.reshape(b, nc, L, G, N)
    dtc = dt.reshape(b, nc, L, G, E)
    xdt = x * dtc[..., None]
    acs = jnp.cumsum((dtc * a.reshape(G, E)).transpose(0, 1, 3, 4, 2), axis=-1)
    causal = jnp.tril(jnp.ones((L, L), dtype=bool))
    seg = acs[..., :, None] - acs[..., None, :]
    decay = jnp.exp(jnp.where(causal, seg, -jnp.inf))
    cb = jnp.einsum('bclgn,bcsgn->bcgls', cc, bc, preferred_element_type=jnp.float32)
    y_diag = jnp.einsum('bcgls,bcgels,bcsgep->bclgep', cb, decay, xdt)
    decay_states = jnp.exp(acs[..., -1:] - acs)
    states = jnp.einsum('bclgn,bcgel,bclgep->bcgepn', bc, decay_states, xdt)
    chunk_decay = jnp.exp(acs[..., -1])

    def step(carry, inp):
        st, dec = inp
        return carry * dec[..., None, None] + st, carry

    init = jnp.zeros((b, G, E, P, N), dtype=states.dtype)
    _, prev = lax.scan(step, init, (states.transpose(1, 0, 2, 3, 4, 5),
                                    chunk_decay.transpose(1, 0, 2, 3)))
    prev = prev.transpose(1, 0, 2, 3, 4, 5)
    y_off = jnp.einsum('bclgn,bcgepn,bcgel->bclgep', cc, prev, jnp.exp(acs))
    y = y_diag + y_off + x * d_skip.reshape(G, E)[..., None]
    y = y.reshape(b, s, SSD_WIDTH).astype(z.dtype) * jax.nn.silu(z)
    y = rms_norm(y.reshape(b, s, G, SSD_WIDTH // G), norm_w.reshape(G, SSD_WIDTH // G))
    return y.reshape(b, s, SSD_WIDTH)


def cross_attention(hn, mem_n, w_q, w_kv, q_norm_w, k_norm_w, w_o):
    b, s = hn.shape[:2]
    m = mem_n.shape[1]
    q = (hn @ w_q).reshape(b, s, X_HEADS, X_HEAD_DIM)
    k, v = jnp.split(mem_n @ w_kv, 2, axis=-1)
    k = k.reshape(b, m, X_HEADS, X_HEAD_DIM)
    v = v.reshape(b, m, X_HEADS, X_HEAD_DIM)
    q = rms_norm(q, q_norm_w) * (X_HEAD_DIM ** -0.5)
    k = rms_norm(k, k_norm_w)
    sc = jnp.einsum('bqhd,bkhd->bhqk', q, k, preferred_element_type=jnp.float32)
    p = jax.nn.softmax(sc, axis=-1)
    o = jnp.einsum('bhqk,bkhd->bqhd', p.astype(v.dtype), v).reshape(b, s, X_WIDTH)
    return o @ w_o


def setup_inputs(seed: int = 0) -> dict:
    key = jax.random.key(seed)
    ks = jax.random.split(key, 32)
    f32 = jnp.float32
    nrm = lambda k, shape, scale: jax.random.normal(k, shape, f32) * scale
    gain = lambda k, shape: 1.0 + 0.02 * jax.random.normal(k, shape, f32)
    dt0 = jnp.exp(jax.random.uniform(ks[12], (DEPTH, SSD_HEADS), f32, math.log(1e-3), math.log(1e-1)))
    return {
        "x": nrm(ks[0], (BATCH, SEQ, D_MODEL), 1.0),
        "mem": nrm(ks[1], (BATCH, MEM_LEN, D_MODEL), 1.0),
        "mix_norm_w": gain(ks[2], (DEPTH, D_MODEL)),
        "w_in": nrm(ks[3], (DEPTH, D_MODEL, IN_COLS), D_MODEL ** -0.5),
        "da_q_norm_w": gain(ks[4], (DEPTH, DA_HEAD_DIM)),
        "da_k_norm_w": gain(ks[5], (DEPTH, DA_HEAD_DIM)),
        "da_lambda_q1": nrm(ks[6], (DEPTH, DA_HEAD_DIM), 0.1),
        "da_lambda_k1": nrm(ks[7], (DEPTH, DA_HEAD_DIM), 0.1),
        "da_lambda_q2": nrm(ks[8], (DEPTH, DA_HEAD_DIM), 0.1),
        "da_lambda_k2": nrm(ks[9], (DEPTH, DA_HEAD_DIM), 0.1),
        "da_subln_w": gain(ks[10], (DEPTH, DA_V_DIM)),
        "ssd_conv_w": nrm(ks[11], (DEPTH, SSD_CONV, CONV_DIM), SSD_CONV ** -0.5),
        "ssd_conv_b": nrm(ks[13], (DEPTH, CONV_DIM), 0.02),
        "ssd_dt_bias": dt0 + jnp.log(-jnp.expm1(-dt0)),
        "ssd_a_log": jnp.log(jax.random.uniform(ks[14], (DEPTH, SSD_HEADS), f32, 1.0, 16.0)),
        "ssd_d": gain(ks[15], (DEPTH, SSD_HEADS)),
        "ssd_norm_w": gain(ks[16], (DEPTH, SSD_WIDTH)),
        "w_out": nrm(ks[17], (DEPTH, MIX_WIDTH, D_MODEL), MIX_WIDTH ** -0.5),
        "xattn_norm_w": gain(ks[18], (DEPTH, D_MODEL)),
        "mem_norm_w": gain(ks[19], (DEPTH, D_MODEL)),
        "xattn_w_q": nrm(ks[20], (DEPTH, D_MODEL, X_WIDTH), D_MODEL ** -0.5),
        "xattn_w_kv": nrm(ks[21], (DEPTH, D_MODEL, 2 * X_WIDTH), D_MODEL ** -0.5),
        "xattn_q_norm_w": gain(ks[22], (DEPTH, X_HEAD_DIM)),
        "xattn_k_norm_w": gain(ks[23], (DEPTH, X_HEAD_DIM)),
        "xattn_w_o": nrm(ks[24], (DEPTH, X_WIDTH, D_MODEL), X_WIDTH ** -0.5),
        "ffn_norm_w": gain(ks[25], (DEPTH, D_MODEL)),
        "ffn_w_gate": nrm(ks[26], (DEPTH, D_MODEL, D_FF), D_MODEL ** -0.5),
        "ffn_w_up": nrm(ks[27], (DEPTH, D_MODEL, D_FF), D_MODEL ** -0.5),
        "ffn_w_down": nrm(ks[28], (DEPTH, D_FF, D_MODEL), D_FF ** -0.5),
    }


def reference(x, mem, mix_norm_w, w_in, da_q_norm_w, da_k_norm_w, da_lambda_q1, da_lambda_k1,
              da_lambda_q2, da_lambda_k2, da_subln_w, ssd_conv_w, ssd_conv_b, ssd_dt_bias,
              ssd_a_log, ssd_d, ssd_norm_w, w_out, xattn_norm_w, mem_norm_w, xattn_w_q,
              xattn_w_kv, xattn_q_norm_w, xattn_k_norm_w, xattn_w_o, ffn_norm_w, ffn_w_gate,
              ffn_w_up, ffn_w_down):
    b, s, _ = x.shape
    h = x
    for i in range(DEPTH):
        lambda_init = 0.8 - 0.6 * math.exp(-0.3 * i)
        hn = rms_norm(h, mix_norm_w[i])
        proj = hn @ w_in[i]
        q, k, v, z, xbc, dt_raw = jnp.split(proj, SPLITS, axis=-1)
        q = q.reshape(b, s, DA_HEADS, 2, DA_HEAD_DIM)
        k = k.reshape(b, s, DA_HEADS, 2, DA_HEAD_DIM)
        v = v.reshape(b, s, DA_HEADS, DA_V_DIM)
        lam = (jnp.exp(jnp.sum(da_lambda_q1[i].astype(jnp.float32) * da_lambda_k1[i].astype(jnp.float32)))
               - jnp.exp(jnp.sum(da_lambda_q2[i].astype(jnp.float32) * da_lambda_k2[i].astype(jnp.float32)))
               + lambda_init)
        a_out = diff_attention(q, k, v, da_q_norm_w[i], da_k_norm_w[i], lam, da_subln_w[i], lambda_init)
        s_out = ssd_mixer(z, xbc, dt_raw, ssd_conv_w[i], ssd_conv_b[i], ssd_dt_bias[i],
                          ssd_a_log[i], ssd_d[i], ssd_norm_w[i])
        h = h + jnp.concatenate([a_out, s_out], axis=-1) @ w_out[i]
        mem_n = rms_norm(mem, mem_norm_w[i])
        h = h + cross_attention(rms_norm(h, xattn_norm_w[i]), mem_n, xattn_w_q[i], xattn_w_kv[i],
                                xattn_q_norm_w[i], xattn_k_norm_w[i], xattn_w_o[i])
        hn = rms_norm(h, ffn_norm_w[i])
        h = h + (jax.nn.silu(hn @ ffn_w_gate[i]) * (hn @ ffn_w_up[i])) @ ffn_w_down[i]
    return h
```

```python
import numpy as np
import ml_dtypes
from contextlib import ExitStack
import concourse.bass as bass
import concourse.mybir as mybir
from concourse.bass_utils import run_bass_kernel_spmd

F32 = mybir.dt.float32
BF16 = mybir.dt.bfloat16
ALU = mybir.AluOpType
AF = mybir.ActivationFunctionType
EPS = 1e-6

SAME_ENGINE_SYNC = True
RELAX_SAME_ENGINE_WAW = True


class Buf:
    __slots__ = ("name", "w", "r", "excl")

    def __init__(self, name, excl=False):
        self.name = name
        self.w = None
        self.r = []
        self.excl = excl


class Prog:
    ENGS = ("pe", "act", "dve", "pool", "sp")

    def __init__(self, nc, stack, n_dma_sems=14):
        self.nc = nc
        self.sems = {}
        self.count = {}
        self.known = {e: {} for e in self.ENGS}
        self.ops = {e: [] for e in self.ENGS}
        for e in self.ENGS[:4]:
            self.sems[e] = stack.enter_context(nc.semaphore("s_" + e))
            self.count[e] = 0
        for i in range(n_dma_sems):
            k = "dma%d" % i
            self.sems[k] = stack.enter_context(nc.semaphore("s_" + k))
            self.count[k] = 0
        self.n_ops = 0

    def _need(self, eng, dep, waits):
        if dep is None:
            return
        k, v = dep
        if k == eng and (eng == "pe" or not SAME_ENGINE_SYNC):
            return
        if self.known[eng].get(k, 0) >= v:
            return
        if v > waits.get(k, 0):
            waits[k] = v

    def _deps(self, eng, reads, writes):
        waits = {}
        for b in reads:
            self._need(eng, b.w, waits)
            if b.excl:
                for r in b.r:
                    if r[0] != eng:
                        self._need(eng, r, waits)
        for b in writes:
            if not (b.w is not None and b.w[0] == eng and RELAX_SAME_ENGINE_WAW):
                self._need(eng, b.w, waits)
            for r in b.r:
                if r[0] == eng and RELAX_SAME_ENGINE_WAW:
                    continue
                self._need(eng, r, waits)
        for k, v in waits.items():
            self.known[eng][k] = v
        return list(waits.items())

    def _mark(self, tok, reads, writes):
        for b in reads:
            b.r.append(tok)
            if len(b.r) > 48:
                mx = {}
                for k, v in b.r:
                    if v > mx.get(k, 0):
                        mx[k] = v
                b.r = list(mx.items())
        for b in writes:
            b.w = tok
            b.r = []

    def op(self, eng, fn, reads=(), writes=()):
        waits = self._deps(eng, reads, writes)
        self.count[eng] += 1
        tok = (eng, self.count[eng])
        self.ops[eng].append((waits, fn, eng, 1))
        self._mark(tok, reads, writes)
        self.n_ops += 1
        return tok

    def dma(self, queue, fn, semkey, reads=(), writes=()):
        waits = self._deps(queue, reads, writes)
        self.count[semkey] += 16
        tok = (semkey, self.count[semkey])
        self.ops[queue].append((waits, fn, semkey, 16))
        self._mark(tok, reads, writes)
        self.n_ops += 1
        return tok

    def barrier(self):
        for e in self.ENGS:
            self.wait_all(e)

    def wait_all(self, eng):
        waits = []
        for k in self.sems:
            v = self.count[k]
            if k == eng and eng == "pe":
                continue
            if v > self.known[eng].get(k, 0):
                waits.append((k, v))
                self.known[eng][k] = v
        if waits:
            self.ops[eng].append((waits, None, None, 0))

    def emit(self):
        nc = self.nc
        P = self
        with nc.Block() as block:
            def run(ename):
                def body(e):
                    for waits, fn, semkey, inc in P.ops[ename]:
                        for k, v in waits:
                            e.wait_ge(P.sems[k], v)
                        if fn is not None:
                            fn(e).then_inc(P.sems[semkey], inc)
                return body
            block.tensor(run("pe"))
            block.scalar(run("act"))
            block.vector(run("dve"))
            block.gpsimd(run("pool"))
            block.sync(run("sp"))


class Arena:
    def __init__(self, big_ap, words):
        self.big = big_ap
        self.words = words
        self.off = 0
        self.marks = []
        self.peak = 0

    def alloc(self, name, free_shape, dtype, parts=128):
        n = int(np.prod(free_shape))
        esz = 4 if dtype == F32 else 2
        w = (n * esz + 3) // 4
        w = (w + 7) // 8 * 8
        assert self.off + w <= self.words, "SBUF arena overflow at %s: %d+%d>%d" % (
            name, self.off, w, self.words)
        ap = self.big[0:parts, self.off:self.off + w]
        self.off += w
        self.peak = max(self.peak, self.off)
        if dtype != F32:
            ap = ap.bitcast(dtype)
        ap = ap[:, 0:n]
        if len(free_shape) == 2:
            ap = ap.rearrange("p (a b) -> p a b", a=free_shape[0])
        elif len(free_shape) == 3:
            ap = ap.rearrange("p (a b c) -> p a b c", a=free_shape[0], b=free_shape[1])
        return ap, Buf(name)

    def push(self):
        self.marks.append(self.off)

    def pop(self):
        self.off = self.marks.pop()


ARENA_WORDS = 51 * 1024


def build_program(dbg=None):
    nc = bass.Bass("TRN2", target_bir_lowering=False)
    d = {}

    def din(name, shape, dt=F32):
        d[name] = nc.dram_tensor(name, list(shape), dt, kind="ExternalInput").ap()

    din("xT", [2048, 2048]); din("memT", [2048, 256]); din("flag", [128, 1])
    din("w_in", [2048, 8224]); din("w_out", [3072, 2048])
    din("w_xq", [2048, 512]); din("w_xkv", [2048, 1024]); din("w_xo", [512, 2048])
    din("w_g", [2048, 5632]); din("w_u", [2048, 5632]); din("w_d", [5632, 2048])
    din("nw", [128, 5, 16])
    din("pp", [128, 8])
    din("rows", [1, 352])
    din("convw", [128, 24, 4]); din("convb", [128, 24]); din("convb_row", [1, 3072])
    din("c_bf", [128, 5, 128], BF16)
    din("c_f32", [128, 3, 128])
    outT = nc.dram_tensor("outT", [2048, 1024], F32, kind="ExternalOutput").ap()
    dbg_out = None
    if dbg is not None:
        dbg_out = nc.dram_tensor("dbg", [128, 4096], F32, kind="ExternalOutput").ap()

    with ExitStack() as st:
        big = st.enter_context(nc.sbuf_tensor("big", [128, ARENA_WORDS], F32))
        ps_t = [st.enter_context(nc.psum_tensor("ps%d" % i, [128, 512], F32)) for i in range(8)]
        ps = [t[:, :] for t in ps_t]
        psb = [Buf("ps%d" % i, excl=True) for i in range(8)]
        P = Prog(nc, st)
        A = Arena(big, ARENA_WORDS)

        def mm(out, lhsT, rhs, start, stop, reads, writes):
            P.op("pe", lambda e: e.matmul(out, lhsT=lhsT, rhs=rhs, start=start, stop=stop),
                 reads=reads, writes=writes)

        def act(out, in_, func, reads, writes, scale=1.0, bias=0.0, accum_out=None):
            if accum_out is None:
                P.op("act", lambda e: e.activation(out=out, in_=in_, func=func, bias=bias, scale=scale),
                     reads=reads, writes=writes)
            else:
                P.op("act", lambda e: e.activation(out=out, in_=in_, func=func, bias=bias, scale=scale,
                                                   accum_out=accum_out), reads=reads, writes=writes)

        def tt(out, in0, in1, op, reads, writes, eng="dve"):
            P.op(eng, lambda e: e.tensor_tensor(out=out, in0=in0, in1=in1, op=op), reads=reads, writes=writes)

        def stt(out, in0, scalar, in1, op0, op1, reads, writes):
            P.op("dve", lambda e: e.scalar_tensor_tensor(out=out, in0=in0, scalar=scalar, in1=in1, op0=op0, op1=op1),
                 reads=reads, writes=writes)

        def ts(out, in0, s1, op0, reads, writes, s2=None, op1=None, eng="dve"):
            if op1 is None:
                P.op(eng, lambda e: e.tensor_scalar(out=out, in0=in0, scalar1=s1, scalar2=None, op0=op0),
                     reads=reads, writes=writes)
            else:
                P.op(eng, lambda e: e.tensor_scalar(out=out, in0=in0, scalar1=s1, scalar2=s2, op0=op0, op1=op1),
                     reads=reads, writes=writes)

        def cp(out, in_, reads, writes, eng="dve"):
            P.op(eng, lambda e: e.tensor_copy(out=out, in_=in_), reads=reads, writes=writes)

        def memset(ap, val, writes, eng="dve"):
            P.op(eng, lambda e: e.memset(ap, val), writes=writes)

        def dma(queue, out, in_, key, reads=(), writes=()):
            P.dma(queue, lambda e: e.dma_start(out=out, in_=in_), key, reads=reads, writes=writes)

        dbg_state = {"done": False}
        hn_off = 0

        def dump(ap2d, buf, ncols):
            if A.off + ncols > A.words:
                P.barrier()
                A.off = hn_off
            stage, sb = A.alloc("dbgstage", [ncols], F32)
            cp(stage, ap2d, [buf], [sb])
            dma("sp", dbg_out[:, 0:ncols], stage, "dma13", reads=[sb])
            dbg_state["done"] = True

        def finish():
            P.wait_all("sp")
            P.emit()

        cbf, cbf_b = A.alloc("cbf", [5, 128], BF16)
        cf, cf_b = A.alloc("cf", [3, 128], F32)
        nw, nw_b = A.alloc("nw", [5, 16], F32)
        pp, pp_b = A.alloc("pp", [8], F32)
        rows, rows_b = A.alloc("rows", [352], F32)
        convw, convw_b = A.alloc("convw", [24, 4], F32)
        convb, convb_b = A.alloc("convb", [24], F32)
        flag, flag_b = A.alloc("flag", [1], F32)
        sm, sm_b = A.alloc("sm", [16], F32)
        onesflag, onesflag_b = A.alloc("onesflag", [128], BF16)
        consts = [(cbf, d["c_bf"], cbf_b), (cf, d["c_f32"], cf_b), (nw, d["nw"], nw_b), (pp, d["pp"], pp_b),
                  (rows, d["rows"].partition_broadcast(128), rows_b), (convw, d["convw"], convw_b),
                  (convb, d["convb"], convb_b), (flag, d["flag"], flag_b)]
        for (dst, src, b) in consts:
            dma("sp", dst, src, "dma0", writes=[b])
        fin = ("dma0", P.count["dma0"])
        for (_, _, b) in consts:
            b.w = fin
        ident = cbf[:, 0, :]; ones_bf = cbf[:, 1, :]; blockones = cbf[:, 2, :]; tri_bf = cbf[:, 3, :]; SL_bf = cbf[:, 4, :]
        triU = cf[:, 0, :]; SLm = cf[:, 1, :]; ones_f = cf[:, 2, :]
        ts(sm[:, 0:1], pp[:, 0:1], 0.125, ALU.mult, [pp_b], [sm_b])
        cp(sm[:, 1:2], pp[:, 1:2], [pp_b], [sm_b])
        ts(sm[:, 2:3], pp[:, 2:3], 0.8, ALU.mult, [pp_b], [sm_b])
        ts(sm[:, 3:4], pp[:, 3:4], float(128 ** -0.5), ALU.mult, [pp_b], [sm_b])
        cp(sm[:, 4:5], pp[:, 4:5], [pp_b], [sm_b])
        junk, junk_b = A.alloc("junk64", [64], F32)
        stt(junk, rows[:, 0:64], 1.0, rows[:, 64:128], ALU.mult, ALU.mult, [rows_b], [junk_b])
        P.op("dve", lambda e: e.tensor_reduce(out=sm[:, 6:7], in_=junk, axis=mybir.AxisListType.X, op=ALU.add),
             reads=[junk_b], writes=[sm_b])
        stt(junk, rows[:, 128:192], 1.0, rows[:, 192:256], ALU.mult, ALU.mult, [rows_b, sm_b], [junk_b])
        P.op("dve", lambda e: e.tensor_reduce(out=sm[:, 7:8], in_=junk, axis=mybir.AxisListType.X, op=ALU.add),
             reads=[junk_b], writes=[sm_b])
        act(sm[:, 6:8], sm[:, 6:8], AF.Exp, [sm_b], [sm_b])
        tt(sm[:, 5:6], sm[:, 6:7], sm[:, 7:8], ALU.subtract, [sm_b], [sm_b])
        ts(sm[:, 5:6], sm[:, 5:6], 0.2, ALU.add, [sm_b], [sm_b])
        ts(onesflag, ones_bf, flag[:, 0:1], ALU.mult, [cbf_b, flag_b], [onesflag_b])
        abc, abc_b = A.alloc("abc", [32], F32)
        act(abc, rows[:, 288:320], AF.Exp, [rows_b], [abc_b])
        ts(abc, abc, -1.0, ALU.mult, [abc_b], [abc_b])
        dtbias_bc = rows[:, 256:288]
        D_bc = rows[:, 320:352]

        if dbg == "c0":
            dump(sm, sm_b, 16)
            finish(); return nc
        SLOT_ELEMS = 4096
        ring = {"slots": [], "i": 0}

        def make_ring(n, specs):
            ring["slots"] = []
            for i in range(n):
                ap, b = A.alloc("wslot%d" % i, [SLOT_ELEMS], BF16)
                ring["slots"].append((ap, b, "dma%d" % (1 + i)))
            ring["specs"] = specs
            ring["issued"] = 0
            ring["released"] = [False] * len(specs)
            ring["tiles"] = {}
            _pump()

        def _pump():
            n = len(ring["slots"])
            while ring["issued"] < len(ring["specs"]) and (
                    ring["issued"] < n or ring["released"][ring["issued"] - n]):
                i = ring["issued"]
                src_ap, kc, ncols = ring["specs"][i]
                ap, b, key = ring["slots"][i % n]
                assert kc * ncols <= SLOT_ELEMS
                view = ap[:, 0:kc * ncols].rearrange("p (k c) -> p k c", k=kc)
                dma("pool", view, src_ap, key, writes=[b])
                ring["tiles"][i] = (view, b)
                ring["issued"] += 1

        def wget(i):
            _pump()
            assert i in ring["tiles"], "weight tile %d not loadable yet (slot predecessor not released)" % i
            return ring["tiles"][i]

        def wrel(i):
            ring["released"][i] = True
            _pump()

        def wsrc(name, c0, ncols, r0=0, nrows=None):
            t = d[name]
            if nrows is None:
                nrows = t.shape[0]
            return t[r0:r0 + nrows, c0:c0 + ncols].rearrange("(k p) c -> p k c", p=128)

        def rstd_from_ps(ps_ap, ps_buf, out_ap, out_buf, inv_n):
            act(out_ap, ps_ap, AF.Ln, [ps_buf], [out_buf], scale=inv_n, bias=EPS)
            act(out_ap, out_ap, AF.Exp, [out_buf], [out_buf], scale=-0.5)

        sqr = []
        for i in range(2):
            sqr.append(A.alloc("sq%d" % i, [512], BF16))
        rstd_t, rstd_b = A.alloc("rstd", [512], F32)
        sq_i = {"i": 0}

        def rmsnorm_T(src, src_b, n, wcol, dst_fn, dst_b, bank):
            for c in range(16):
                sq, sq_b = sqr[sq_i["i"] % 2]
                sq_i["i"] += 1
                act(sq[:, 0:n], src[:, c, :], AF.Square, [src_b], [sq_b])
                mm(ps[bank][:, 0:n], ones_bf, sq[:, 0:n], c == 0, c == 15, [cbf_b, sq_b], [psb[bank]])
            if dbg == "a1":
                dump(ps[bank][:, 0:n], psb[bank], n)
                return
            rstd_from_ps(ps[bank][:, 0:n], psb[bank], rstd_t[:, 0:n], rstd_b, 1.0 / 2048)
            if dbg == "a2":
                dump(rstd_t[:, 0:n], rstd_b, n)
                return
            for c in range(16):
                stt(dst_fn(c), src[:, c, :], wcol(c), rstd_t[:, 0:n], ALU.mult, ALU.mult,
                    [src_b, nw_b, rstd_b], [dst_b])

        hn_off = A.off
        hnT, hnT_b = A.alloc("hnT", [16, 2048], BF16)
        mix_off = A.off
        mixT, mixT_b = A.alloc("mixT", [24, 1024], BF16)
        A.push()
        xst = []
        for i in range(2):
            ap, b = A.alloc("xst%d" % i, [16, 512], F32)
            xst.append((ap, b, "dma%d" % (7 + i)))
        xTv = d["xT"].rearrange("(c p) t -> p c t", p=128)
        for t4 in range(4):
            xs, xs_b, key = xst[t4 % 2]
            for q in range(4):
                dma("sp", xs[:, 4 * q:4 * q + 4, :], xTv[:, 4 * q:4 * q + 4, t4 * 512:(t4 + 1) * 512], key, writes=[xs_b])
            rmsnorm_T(xs, xs_b, 512, lambda c: nw[:, 0, c:c + 1],
                      lambda c, t4=t4: hnT[:, c, t4 * 512:(t4 + 1) * 512], hnT_b, t4 % 2)
            if dbg in ("a1", "a2"):
                finish(); return nc
            if dbg == "a3":
                dump(hnT[:, 3, 0:512], hnT_b, 512)
                finish(); return nc
        A.pop()
        P.barrier()
        if dbg == "hn":
            dump(hnT[:, 3, 1024:2048], hnT_b, 1024)
            finish(); return nc

        A.push()
        specsB = [(wsrc("w_in", 8192, 32), 16, 32)]
        for g_ in range(4):
            specsB += [(wsrc("w_in", 5120 + 512 * g_, 256), 16, 256), (wsrc("w_in", 5120 + 512 * g_ + 256, 256), 16, 256),
                       (wsrc("w_in", 7168 + 128 * g_, 128), 16, 128), (wsrc("w_in", 7680 + 128 * g_, 128), 16, 128),
                       (wsrc("w_in", 3072 + 512 * g_, 256), 16, 256), (wsrc("w_in", 3072 + 512 * g_ + 256, 256), 16, 256)]
        make_ring(3, specsB)
        dt_all, dt_b = A.alloc("dt_all", [16, 32], F32)
        adt_all, adt_b = A.alloc("adt_all", [16, 32], F32)
        E_all, E_b = A.alloc("E_all", [16, 32], F32)
        dtds_all, dtds_b = A.alloc("dtds_all", [16, 32], F32)
        cd_all, cd_b = A.alloc("cd_all", [16, 32], F32)
        rseg_hi, rseg_b = A.alloc("rseg_hi", [8, 128], BF16)
        rseg_lo, _ = A.alloc("rseg_lo", [8, 128], BF16)
        tmpd, tmpd_b = A.alloc("tmpd", [16, 32], F32)
        adt_hi, adth_b = A.alloc("adt_hi", [16, 32], BF16)
        adt_lo, _ = A.alloc("adt_lo", [16, 32], BF16)
        wdt, wdt_b = wget(0)
        for tb in range(16):
            for kc in range(16):
                mm(ps[0][:, tb * 32:(tb + 1) * 32], hnT[:, kc, tb * 128:(tb + 1) * 128], wdt[:, kc, :],
                   kc == 0, kc == 15, [hnT_b, wdt_b], [psb[0]])
        if dbg == "dt0":
            dump(ps[0], psb[0], 512)
            finish(); return nc
        wrel(0)
        ps0_3 = ps[0].rearrange("p (a b) -> p a b", a=16)
        tt(tmpd, ps0_3, dtbias_bc.unsqueeze(1).to_broadcast([128, 16, 32]), ALU.add, [psb[0], rows_b], [tmpd_b])
        act(tmpd, tmpd, AF.Exp, [tmpd_b], [tmpd_b])
        act(dt_all, tmpd, AF.Ln, [tmpd_b], [dt_b], bias=1.0)
        tt(adt_all, dt_all, abc.unsqueeze(1).to_broadcast([128, 16, 32]), ALU.mult, [dt_b, abc_b], [adt_b])
        if dbg == "dt1":
            dump(dt_all.rearrange("p a b -> p (a b)"), dt_b, 512)
            finish(); return nc
        adt_flat = adt_all.rearrange("p a b -> p (a b)")
        cp(adt_hi, adt_all, [adt_b], [adth_b])
        tt(adt_lo, adt_all, adt_hi, ALU.subtract, [adt_b, adth_b], [adth_b])
        ahf = adt_hi.rearrange("p a b -> p (a b)")
        alf = adt_lo.rearrange("p a b -> p (a b)")
        mm(ps[1], tri_bf, ahf, True, False, [cbf_b, adth_b], [psb[1]])
        mm(ps[1], tri_bf, alf, False, True, [cbf_b, adth_b], [psb[1]])
        mm(ps[2], ones_bf, ahf, True, False, [cbf_b, adth_b], [psb[2]])
        mm(ps[2], ones_bf, alf, False, True, [cbf_b, adth_b], [psb[2]])
        act(E_all.rearrange("p a b -> p (a b)"), ps[1], AF.Exp, [psb[1]], [E_b])
        cp(tmpd.rearrange("p a b -> p (a b)"), ps[1], [psb[1]], [tmpd_b])
        tt(tmpd.rearrange("p a b -> p (a b)"), ps[2], tmpd.rearrange("p a b -> p (a b)"), ALU.subtract,
           [psb[2], tmpd_b], [tmpd_b])
        act(tmpd, tmpd, AF.Exp, [tmpd_b], [tmpd_b])
        tt(dtds_all, dt_all, tmpd, ALU.mult, [dt_b, tmpd_b], [dtds_b])
        act(cd_all.rearrange("p a b -> p (a b)"), ps[2], AF.Exp, [psb[2]], [cd_b])
        if dbg == "dt":
            dump(dtds_all.rearrange("p a b -> p (a b)"), dtds_b, 512)
            finish(); return nc

        diagWs = [A.alloc("diagW%d" % i, [4, 128], BF16) for i in range(2)]
        xpre = [A.alloc("xpre%d" % i, [2051], BF16) for i in range(1)]
        for ap, b in xpre:
            memset(ap[:, 0:3], 0.0, [b])
        x_tok = mixT[:, 0:8, :].rearrange("p a (b c) -> p (a b) c", c=512)
        x_tok_b = Buf("x_tok")
        B_tok, B_tok_b = A.alloc("B_tok", [16, 128], BF16)
        xsT, xsT_b = A.alloc("xsT", [2048], BF16)
        BT, BT_b = A.alloc("BT", [1024], BF16)
        CT, CT_b = A.alloc("CT", [1024], BF16)
        zs, zs_b = A.alloc("zs", [8, 512], BF16)
        S, S_b = A.alloc("S", [8, 64], F32)
        S_bf, S_bf_b = A.alloc("S_bf", [2, 512], BF16)
        xdtds, xdtds_b = A.alloc("xdtds", [8, 64], BF16)
        xdt, xdt_b = A.alloc("xdt", [8, 64], BF16)
        xD, xD_b = A.alloc("xD", [8, 64], BF16)
        CBm, CBm_b = A.alloc("CBm", [128], BF16)
        MT, MT_b = A.alloc("MT", [8, 128], BF16)
        t1, t1_b = A.alloc("t1", [8, 64], F32)
        yn, yn_b = A.alloc("yn", [512], BF16)
        ssv, ssv_b = A.alloc("ssv", [2], F32)

        for g in range(4):
            hs = slice(8 * g, 8 * g + 8)
            chunks = [4 * g, 4 * g + 1, 4 * g + 2, 4 * g + 3, 16 + g, 20 + g]
            wbase = 1 + 6 * g
            for ci in range(6):
                xp, xp_b = xpre[0]
                diagW, diagW_b = diagWs[ci % 2]
                for j in range(4):
                    ts(diagW[:, j, :], ident, convw[:, chunks[ci], j:j + 1], ALU.mult, [cbf_b, convw_b], [diagW_b])
                widx = wbase + (ci // 2 if ci < 4 else ci - 2)
                (w_ap, w_b), c0 = wget(widx), ((ci % 2) * 128 if ci < 4 else 0)
                tts = (1, 2, 3) if ci == 5 else (0, 1, 2, 3)
                for t4 in tts:
                    bank = t4 % 2
                    for kc in range(16):
                        mm(ps[bank], w_ap[:, kc, c0:c0 + 128], hnT[:, kc, t4 * 512:(t4 + 1) * 512],
                           kc == 0, kc == 15, [w_b, hnT_b], [psb[bank]])
                    act(xp[:, 3 + t4 * 512:3 + (t4 + 1) * 512], ps[bank], AF.Copy, [psb[bank]], [xp_b])
                if ci in (1, 3, 4, 5):
                    wrel(widx)
                if ci < 5:
                    for t4 in range(4):
                        bank = 2 + t4 % 2
                        for j in range(4):
                            mm(ps[bank], diagW[:, j, :], xp[:, t4 * 512 + j:t4 * 512 + j + 512],
                               j == 0, j == 3, [diagW_b, xp_b], [psb[bank]])
                        act(xsT[:, t4 * 512:(t4 + 1) * 512], ps[bank], AF.Silu, [psb[bank], convb_b], [xsT_b],
                            bias=convb[:, chunks[ci]:chunks[ci] + 1])
                    dst = x_tok if ci < 4 else B_tok
                    dst_b = x_tok_b if ci < 4 else B_tok_b
                    dc0 = ci * 128 if ci < 4 else 0
                    for tq in range(4):
                        bank = 4 + tq % 2
                        pstx = ps[bank].bitcast(BF16)
                        for i in range(4):
                            tb = tq * 4 + i
                            P.op("pe", lambda e, i=i, tb=tb, pstx=pstx: e.transpose(
                                out=pstx[:, i * 128:(i + 1) * 128], in_=xsT[:, tb * 128:(tb + 1) * 128], identity=ident),
                                reads=[xsT_b, cbf_b], writes=[psb[bank]])
                        cp(dst[:, tq * 4:tq * 4 + 4, dc0:dc0 + 128],
                           pstx[:, 0:512].rearrange("p (a b) -> p a b", a=4), [psb[bank]], [dst_b])
                    if ci == 4:
                        cp(BT, xsT[:, 1024:2048], [xsT_b], [BT_b])
                if ci == 5:
                    for t in range(2):
                        bank = 2 + t
                        for j in range(4):
                            mm(ps[bank], diagW[:, j, :], xp[:, 1024 + t * 512 + j:1024 + t * 512 + j + 512],
                               j == 0, j == 3, [diagW_b, xp_b], [psb[bank]])
                        act(CT[:, t * 512:(t + 1) * 512], ps[bank], AF.Silu, [psb[bank], convb_b], [CT_b],
                            bias=convb[:, chunks[ci]:chunks[ci] + 1])
            wzh = [wget(wbase + 4), wget(wbase + 5)]

            def zproj(tb, wzh=wzh):
                bank = 2 if tb % 2 == 0 else 1
                for i in range(2):
                    for kc in range(16):
                        mm(ps[bank][:, 256 * i:256 * i + 256], hnT[:, kc, 1024 + tb * 128:1024 + (tb + 1) * 128],
                           wzh[i][0][:, kc, :], kc == 0, kc == 15, [hnT_b, wzh[i][1]], [psb[bank]])
                act(zs[:, tb, :], ps[bank], AF.Silu, [psb[bank]], [zs_b])
            if dbg == "xtok" and g == 0:
                dump(x_tok[:, 9, :], x_tok_b, 512)
                finish(); return nc
            if dbg == "ct" and g == 0:
                dump(CT, CT_b, 1024)
                finish(); return nc
            memset(S, 0.0, [S_b])
            memset(S_bf, 0.0, [S_bf_b])

            def bc(ap2):
                return ap2.unsqueeze(2).to_broadcast([128, 8, 64])

            def stA(c):
                own = c >= 8
                x3 = x_tok[:, c, :].rearrange("p (h j) -> p h j", h=8)
                if own:
                    tl = (c - 8) * 128
                    tt(rseg_hi, tri_bf.unsqueeze(1).to_broadcast([128, 8, 128]),
                       adt_hi[:, c, hs].unsqueeze(2).to_broadcast([128, 8, 128]), ALU.mult,
                       [cbf_b, adth_b], [rseg_b], eng="pool")
                    tt(rseg_lo, tri_bf.unsqueeze(1).to_broadcast([128, 8, 128]),
                       adt_lo[:, c, hs].unsqueeze(2).to_broadcast([128, 8, 128]), ALU.mult,
                       [cbf_b, adth_b], [rseg_b], eng="pool")
                if c < 15:
                    tt(xdtds, x3, bc(dtds_all[:, c, hs]), ALU.mult, [x_tok_b, dtds_b], [xdtds_b])
                if own:
                    tt(xdt, x3, bc(dt_all[:, c, hs]), ALU.mult, [x_tok_b, dt_b], [xdt_b], eng="pool")
                    tt(xD, x3, bc(D_bc[:, hs]), ALU.mult, [x_tok_b, rows_b], [xD_b])
                    mm(ps[6][:, 0:128], BT[:, tl:tl + 128], CT[:, tl:tl + 128], True, True,
                       [BT_b, CT_b], [psb[6]])
                    for hh in range(2):
                        mm(ps[hh], SL_bf, rseg_hi[:, 4 * hh:4 * hh + 4, :].rearrange("p a b -> p (a b)"), True, False,
                           [cbf_b, rseg_b], [psb[hh]])
                        mm(ps[hh], SL_bf, rseg_lo[:, 4 * hh:4 * hh + 4, :].rearrange("p a b -> p (a b)"), False, True,
                           [cbf_b, rseg_b], [psb[hh]])
                        act(MT[:, 4 * hh:4 * hh + 4, :].rearrange("p a b -> p (a b)"), ps[hh], AF.Exp,
                            [psb[hh]], [MT_b])
                if c < 15:
                    mm(ps[4], B_tok[:, c, :], xdtds.rearrange("p h j -> p (h j)"), True, True,
                       [B_tok_b, xdtds_b], [psb[4]])

            def stB(c):
                own = c >= 8
                if own:
                    tl = (c - 8) * 128
                    tt(CBm, ps[6][:, 0:128], triU, ALU.mult, [psb[6], cf_b], [CBm_b])
                    tt(MT, MT, CBm.unsqueeze(1).to_broadcast([128, 8, 128]), ALU.mult, [MT_b, CBm_b], [MT_b])
                    mm(ps[7], ident, xD.rearrange("p h j -> p (h j)"), True, False, [cbf_b, xD_b], [psb[7]])
                    for h in range(8):
                        mm(ps[7][:, h * 64:(h + 1) * 64], MT[:, h, :], xdt[:, h, :], False, h == 7,
                           [MT_b, xdt_b], [psb[7]])
                    mm(ps[5], CT[:, tl:tl + 128], S_bf[:, c % 2, :], True, True, [CT_b, S_bf_b], [psb[5]])
                if c < 15:
                    tt(S, S, bc(cd_all[:, c, hs]), ALU.mult, [S_b, cd_b], [S_b], eng="pool")
                    Sf = S.rearrange("p h j -> p (h j)")
                    tt(Sf, Sf, ps[4], ALU.add, [S_b, psb[4]], [S_b])
                    if c == 7:
                        ts(Sf, Sf, flag[:, 0:1], ALU.mult, [S_b, flag_b], [S_b])
                    if c >= 7:
                        act(S_bf[:, (c + 1) % 2, :], Sf, AF.Copy, [S_b], [S_bf_b])

            def stT1(c):
                if c < 8:
                    return
                tt(t1, ps[5].rearrange("p (h j) -> p h j", h=8), bc(E_all[:, c, hs]), ALU.mult,
                   [psb[5], E_b], [t1_b])
                t1f = t1.rearrange("p h j -> p (h j)")
                tt(t1f, t1f, ps[7], ALU.add, [t1_b, psb[7]], [t1_b])
                tt(t1f, t1f, zs[:, c - 8, :], ALU.mult, [t1_b, zs_b], [t1_b])
                act(yn, t1f, AF.Square, [t1_b], [yn_b, ssv_b], accum_out=ssv[:, 0:1])
                act(ssv[:, 1:2], ssv[:, 0:1], AF.Ln, [ssv_b], [ssv_b], scale=1.0 / 512, bias=EPS)
                act(ssv[:, 1:2], ssv[:, 1:2], AF.Exp, [ssv_b], [ssv_b], scale=-0.5)

            def stT2(c):
                if c < 8:
                    return
                tl = (c - 8) * 128
                t1f = t1.rearrange("p h j -> p (h j)")
                act(yn, t1f, AF.Copy, [t1_b, ssv_b], [yn_b], scale=ssv[:, 1:2])
                pst = ps[3].bitcast(BF16)
                for j in range(4):
                    P.op("pe", lambda e, j=j, pst=pst: e.transpose(out=pst[:, j * 128:(j + 1) * 128],
                                                                 in_=yn[:, j * 128:(j + 1) * 128], identity=ident),
                         reads=[yn_b, cbf_b], writes=[psb[3]])
                tt(mixT[:, 8 + 4 * g:12 + 4 * g, tl:tl + 128],
                   pst[:, 0:512].rearrange("p (a b) -> p a b", a=4),
                   nw[:, 4, 4 * g:4 * g + 4].unsqueeze(2).to_broadcast([128, 4, 128]), ALU.mult,
                   [psb[3], nw_b], [mixT_b])

            stA(0)
            stB(0)
            for c in range(16):
                if c < 8:
                    zproj(c)
                    if c == 7:
                        wrel(wbase + 4); wrel(wbase + 5)
                if c + 1 <= 15:
                    stA(c + 1)
                stT1(c)
                if c + 1 <= 15:
                    stB(c + 1)
                stT2(c)
            if dbg == "ssd" and g == 0:
                dump(mixT[:, 8, :], mixT_b, 1024)
                finish(); return nc
        A.pop()
        P.barrier()

        A.push()
        specsC = []
        for h_ in range(8):
            specsC += [(wsrc("w_in", h_ * 128, 128), 16, 128), (wsrc("w_in", 1024 + h_ * 128, 128), 16, 128),
                       (wsrc("w_in", 2048 + h_ * 128, 128), 16, 128)]
        make_ring(5, specsC)
        qT, qT_b = A.alloc("qT", [1024], BF16)
        kT, kT_b = A.alloc("kT", [2048], BF16)
        Vh, Vh_b = A.alloc("Vh", [16, 128], BF16)
        Pb = [[A.alloc("P%d_%d" % (br, i), [512], BF16) for i in range(2)] for br in range(2)]
        R1, R1_b = A.alloc("R1", [512], F32)
        R2, R2_b = A.alloc("R2", [512], F32)
        ob, ob_b = A.alloc("ob", [512], F32)
        ob2, ob2_b = A.alloc("ob2", [512], F32)
        lnq, lnq_b = A.alloc("lnq", [512], F32)
        sqa, sqa_b = A.alloc("sqa", [512], BF16)
        vstage = [A.alloc("vst%d" % i, [512], BF16) for i in range(2)]
        sqn = [(sqa, sqa_b), A.alloc("sqn1", [512], BF16)]
        lnn = [(lnq, lnq_b), A.alloc("lnn1", [512], F32)]
        obfin = [A.alloc("obf%d" % i, [512], F32) for i in range(2)]
        sqf = [A.alloc("sqf%d" % i, [512], BF16) for i in range(2)]
        pending_f2 = []

        def flush_f2():
            if not pending_f2:
                return
            hh_, slot_ = pending_f2.pop(0)
            obf, obf_b = obfin[slot_]
            mm(ps[0], ones_bf, sqf[slot_][0], True, True, [cbf_b, sqf[slot_][1]], [psb[0]])
            rstd_from_ps(ps[0], psb[0], lnq, lnq_b, 1.0 / 128)
            stt(mixT[:, hh_, slot_ * 512:(slot_ + 1) * 512], obf, sm[:, 2:3], lnq, ALU.mult, ALU.mult,
                [obf_b, sm_b, lnq_b], [mixT_b])

        def qknorm(psrc, psrc_b, pss, pss_b, wcol, dst, dst_b, blk, inv_n):
            act(sqa, psrc, AF.Square, [psrc_b], [sqa_b])
            mm(pss, blk, sqa, True, True, [cbf_b, sqa_b], [pss_b])
            rstd_from_ps(pss, pss_b, lnq, lnq_b, inv_n)
            stt(dst, psrc, wcol, lnq, ALU.mult, ALU.mult, [psrc_b, sm_b, lnq_b], [dst_b])

        for h in range(8):
            wq, wq_b = wget(3 * h)
            wk, wk_b = wget(3 * h + 1)
            wv, wv_b = wget(3 * h + 2)
            ntl = [(wq, wq_b, 1024 + t * 512, sm[:, 0:1], qT[:, t * 512:(t + 1) * 512], qT_b) for t in range(2)]
            ntl += [(wk, wk_b, t4 * 512, sm[:, 1:2], kT[:, t4 * 512:(t4 + 1) * 512], kT_b) for t4 in range(4)]

            def nproj(i):
                w_, w_b_, tok0, _, _, _ = ntl[i]
                a = 2 * (i % 2)
                for kc in range(16):
                    mm(ps[a], w_[:, kc, :], hnT[:, kc, tok0:tok0 + 512], kc == 0, kc == 15, [w_b_, hnT_b], [psb[a]])
                sq_, sq_b_ = sqn[i % 2]
                act(sq_, ps[a], AF.Square, [psb[a]], [sq_b_])

            def nfin(i):
                _, _, _, wcol, dst, dst_b = ntl[i]
                a, b2 = 2 * (i % 2), 2 * (i % 2) + 1
                sq_, sq_b_ = sqn[i % 2]
                ln_, ln_b_ = lnn[i % 2]
                mm(ps[b2], blockones, sq_, True, True, [cbf_b, sq_b_], [psb[b2]])
                rstd_from_ps(ps[b2], psb[b2], ln_, ln_b_, 1.0 / 64)
                stt(dst, ps[a], wcol, ln_, ALU.mult, ALU.mult, [psb[a], sm_b, ln_b_], [dst_b])

            nproj(0)
            for i in range(6):
                if i + 1 < 6:
                    nproj(i + 1)
                nfin(i)
            for t4 in range(4):
                bank = 4 + t4 % 2
                for kc in range(16):
                    mm(ps[bank], wv[:, kc, :], hnT[:, kc, t4 * 512:(t4 + 1) * 512], kc == 0, kc == 15,
                       [wv_b, hnT_b], [psb[bank]])
                vst, vst_b = vstage[t4 % 2]
                act(vst, ps[bank], AF.Copy, [psb[bank]], [vst_b])
                tbank = 6 + t4 % 2
                pstv = ps[tbank].bitcast(BF16)
                for i in range(4):
                    P.op("pe", lambda e, i=i, pstv=pstv, vst=vst: e.transpose(
                        out=pstv[:, i * 128:(i + 1) * 128], in_=vst[:, i * 128:(i + 1) * 128], identity=ident),
                        reads=[vst_b, cbf_b], writes=[psb[tbank]])
                src3 = pstv[:, 0:512].rearrange("p (a b) -> p a b", a=4)
                if t4 < 2:
                    ts(Vh[:, t4 * 4:t4 * 4 + 4, :], src3, flag[:, 0:1], ALU.mult, [psb[tbank], flag_b], [Vh_b])
                else:
                    cp(Vh[:, t4 * 4:t4 * 4 + 4, :], src3, [psb[tbank]], [Vh_b])
            wrel(3 * h); wrel(3 * h + 1); wrel(3 * h + 2)
            if dbg == "qk" and h == 0:
                dump(kT, kT_b, 2048)
                finish(); return nc
            for slot in range(2):
                nkb = 8 + 4 * (slot + 1)
                def geom(kb):
                    j = kb - 8 - 4 * slot
                    q0 = j * 128 if j >= 0 else 0
                    return j, q0, kb % 2

                def emit_S(kb):
                    j, q0, bsel = geom(kb)
                    S1, S2 = ps[2 * bsel], ps[2 * bsel + 1]
                    S1b, S2b = psb[2 * bsel], psb[2 * bsel + 1]
                    qs = slice(slot * 512 + q0, slot * 512 + 512)
                    mm(S1[:, q0:512], kT[0:64, kb * 128:(kb + 1) * 128], qT[0:64, qs], True, True, [kT_b, qT_b], [S1b])
                    mm(S2[:, q0:512], kT[64:128, kb * 128:(kb + 1) * 128], qT[64:128, qs], True, True, [kT_b, qT_b], [S2b])
                    (P1, P1b), (P2, P2b) = Pb[0][bsel], Pb[1][bsel]
                    act(P1[:, q0:512], S1[:, q0:512], AF.Exp, [S1b], [P1b])
                    act(P2[:, q0:512], S2[:, q0:512], AF.Exp, [S2b], [P2b])
                    if j >= 0:
                        tt(P1[:, q0:q0 + 128], P1[:, q0:q0 + 128], tri_bf, ALU.mult, [P1b, cbf_b], [P1b])
                        tt(P2[:, q0:q0 + 128], P2[:, q0:q0 + 128], tri_bf, ALU.mult, [P2b, cbf_b], [P2b])

                def emit_PV(kb):
                    j, q0, bsel = geom(kb)
                    (P1, P1b), (P2, P2b) = Pb[0][bsel], Pb[1][bsel]
                    on, on_b = (onesflag, onesflag_b) if kb < 8 else (ones_bf, cbf_b)
                    first, last = kb == 0, kb == nkb - 1
                    mm(ps[4][:, q0:512], Vh[:, kb, :], P1[:, q0:512], first, last, [Vh_b, P1b], [psb[4]])
                    mm(ps[6][:, q0:512], on, P1[:, q0:512], first, last, [on_b, P1b], [psb[6]])
                    mm(ps[5][:, q0:512], Vh[:, kb, :], P2[:, q0:512], first, last, [Vh_b, P2b], [psb[5]])
                    mm(ps[7][:, q0:512], on, P2[:, q0:512], first, last, [on_b, P2b], [psb[7]])

                emit_S(0)
                for kb in range(nkb):
                    if kb + 1 < nkb:
                        emit_S(kb + 1)
                    if kb == 1:
                        flush_f2()
                    emit_PV(kb)
                pending_f2.append((h, slot))
                act(R1, ps[6], AF.Ln, [psb[6]], [R1_b])
                act(R1, R1, AF.Exp, [R1_b], [R1_b], scale=-1.0)
                act(R2, ps[7], AF.Ln, [psb[7]], [R2_b])
                act(R2, R2, AF.Exp, [R2_b], [R2_b], scale=-1.0)
                tt(ob, ps[4], R1, ALU.mult, [psb[4], R1_b], [ob_b])
                stt(ob2, ps[5], sm[:, 5:6], R2, ALU.mult, ALU.mult, [psb[5], sm_b, R2_b], [ob2_b])
                obf, obf_b = obfin[slot]
                tt(obf, ob, ob2, ALU.subtract, [ob_b, ob2_b], [obf_b])
                act(sqf[slot][0], obf, AF.Square, [obf_b], [sqf[slot][1]])
            if h == 7:
                flush_f2()
            if dbg == "attn" and h == 0:
                flush_f2()
                dump(mixT[:, 0, :], mixT_b, 1024)
                finish(); return nc
        A.pop()
        P.barrier()
        h1 = big[:, hn_off:hn_off + 16384].rearrange("p (a b) -> p a b", a=16)
        h1_b = Buf("h1")

        A.push()
        woutv = d["w_out"].rearrange("(k p) c -> p k c", p=128)
        make_ring(6, [(woutv[:, :, o_ * 128:(o_ + 1) * 128], 24, 128) for o_ in range(16)])
        xres = [A.alloc("xres%d" % i, [512], F32) for i in range(2)]
        xi = 0
        woutv = d["w_out"].rearrange("(k p) c -> p k c", p=128)
        for o in range(16):
            wo_t, wo_b = wget(o)
            for t in range(2):
                bank = (2 * o + t) % 4
                xr, xr_b = xres[xi % 2]
                key = "dma%d" % (7 + xi % 2)
                xi += 1
                dma("sp", xr, d["xT"][o * 128:(o + 1) * 128, 1024 + t * 512:1024 + (t + 1) * 512], key, writes=[xr_b])
                for c in range(24):
                    mm(ps[bank], wo_t[:, c, :], mixT[:, c, t * 512:(t + 1) * 512], c == 0, c == 23,
                       [wo_b, mixT_b], [psb[bank]])
                tt(h1[:, o, t * 512:(t + 1) * 512], ps[bank], xr, ALU.add, [psb[bank], xr_b], [h1_b])
            wrel(o)
        A.pop()
        P.barrier()
        if dbg == "h1":
            dump(h1[:, 5, :], h1_b, 1024)
            finish(); return nc

        A.off = mix_off

        A.push()
        woxv = d["w_xo"].rearrange("(k p) c -> p k c", p=128)
        specsE = [(wsrc("w_xkv", 0, 256), 16, 256), (wsrc("w_xkv", 256, 256), 16, 256),
                  (wsrc("w_xkv", 512, 256), 16, 256), (wsrc("w_xkv", 768, 256), 16, 256),
                  (wsrc("w_xq", 0, 256), 16, 256), (wsrc("w_xq", 256, 256), 16, 256),
                  (woxv[:, :, 0:1024], 4, 1024), (woxv[:, :, 1024:2048], 4, 1024)]
        make_ring(6, specsE)
        hxT, hxT_b = A.alloc("hxT", [16, 1024], BF16)
        memn, memn_b = A.alloc("memn", [16, 256], BF16)
        KxT, KxT_b = A.alloc("KxT", [4, 256], BF16)
        Vx, Vx_b = A.alloc("Vx", [2, 512], BF16)
        qx, qx_b = A.alloc("qx", [512], BF16)
        xo, xo_b = A.alloc("xo", [4, 512], BF16)
        Px = [A.alloc("Px%d" % i, [512], BF16) for i in range(2)]
        Rx, Rx_b = A.alloc("Rx", [512], F32)
        lnq, lnq_b = A.alloc("lnq2", [512], F32)
        sqa, sqa_b = A.alloc("sqa2", [512], BF16)
        A.push()
        mst, mst_b = A.alloc("mst", [16, 256], F32)
        dma("sp", mst, d["memT"].rearrange("(c p) t -> p c t", p=128), "dma9", writes=[mst_b])
        rmsnorm_T(mst, mst_b, 256, lambda c: nw[:, 3, c:c + 1], lambda c: memn[:, c, :], memn_b, 0)
        A.pop()
        wkxh = [wget(0), wget(1)]
        for hh in range(4):
            a, b2 = 2 * (hh % 2), 2 * (hh % 2) + 1
            wkx, wkx_b = wkxh[hh // 2]
            for kc in range(16):
                mm(ps[a][:, 0:256], wkx[:, kc, (hh % 2) * 128:(hh % 2) * 128 + 128], memn[:, kc, :], kc == 0, kc == 15,
                   [wkx_b, memn_b], [psb[a]])
            act(sqa[:, 0:256], ps[a][:, 0:256], AF.Square, [psb[a]], [sqa_b])
            mm(ps[b2][:, 0:256], ones_bf, sqa[:, 0:256], True, True, [cbf_b, sqa_b], [psb[b2]])
            rstd_from_ps(ps[b2][:, 0:256], psb[b2], lnq[:, 0:256], lnq_b, 1.0 / 128)
            stt(KxT[:, hh, :], ps[a][:, 0:256], sm[:, 4:5], lnq[:, 0:256], ALU.mult, ALU.mult,
                [psb[a], sm_b, lnq_b], [KxT_b])
        wrel(0); wrel(1)
        wvxh = [wget(2), wget(3)]
        for mb in range(2):
            for i in range(2):
                for kc in range(16):
                    mm(ps[4 + mb][:, 256 * i:256 * i + 256], memn[:, kc, mb * 128:(mb + 1) * 128], wvxh[i][0][:, kc, :],
                       kc == 0, kc == 15, [memn_b, wvxh[i][1]], [psb[4 + mb]])
            cp(Vx[:, mb, :], ps[4 + mb], [psb[4 + mb]], [Vx_b])
        for t in range(2):
            rmsnorm_T(h1[:, :, t * 512:(t + 1) * 512], h1_b, 512, lambda c: nw[:, 1, c:c + 1],
                      lambda c, t=t: hxT[:, c, t * 512:(t + 1) * 512], hxT_b, 6 + t)
        wrel(2); wrel(3)
        wqxh = [wget(4), wget(5)]
        woxh = [wget(6), wget(7)]
        for t in range(2):
            for hh in range(4):
                wqx, wqx_b = wqxh[hh // 2]
                for kc in range(16):
                    mm(ps[0], wqx[:, kc, (hh % 2) * 128:(hh % 2) * 128 + 128], hxT[:, kc, t * 512:(t + 1) * 512],
                       kc == 0, kc == 15, [wqx_b, hxT_b], [psb[0]])
                act(sqa, ps[0], AF.Square, [psb[0]], [sqa_b])
                mm(ps[1], ones_bf, sqa, True, True, [cbf_b, sqa_b], [psb[1]])
                rstd_from_ps(ps[1], psb[1], lnq, lnq_b, 1.0 / 128)
                stt(qx, ps[0], sm[:, 3:4], lnq, ALU.mult, ALU.mult, [psb[0], sm_b, lnq_b], [qx_b])
                for mb in range(2):
                    mm(ps[2 + mb], KxT[:, hh, mb * 128:(mb + 1) * 128], qx, True, True, [KxT_b, qx_b], [psb[2 + mb]])
                    act(Px[mb][0], ps[2 + mb], AF.Exp, [psb[2 + mb]], [Px[mb][1]])
                for mb in range(2):
                    mm(ps[4], Vx[:, mb, hh * 128:(hh + 1) * 128], Px[mb][0], mb == 0, mb == 1,
                       [Vx_b, Px[mb][1]], [psb[4]])
                    mm(ps[5], ones_bf, Px[mb][0], mb == 0, mb == 1, [cbf_b, Px[mb][1]], [psb[5]])
                act(Rx, ps[5], AF.Ln, [psb[5]], [Rx_b])
                act(Rx, Rx, AF.Exp, [Rx_b], [Rx_b], scale=-1.0)
                tt(xo[:, hh, :], ps[4], Rx, ALU.mult, [psb[4], Rx_b], [xo_b])
            for o in range(16):
                bank = 6 + o % 2
                wox, wox_b = woxh[o // 8]
                for hh in range(4):
                    mm(ps[bank], wox[:, hh, (o % 8) * 128:(o % 8) * 128 + 128], xo[:, hh, :], hh == 0, hh == 3,
                       [wox_b, xo_b], [psb[bank]])
                tt(h1[:, o, t * 512:(t + 1) * 512], h1[:, o, t * 512:(t + 1) * 512], ps[bank], ALU.add,
                   [h1_b, psb[bank]], [h1_b])
        A.pop()
        P.barrier()
        if dbg == "h2":
            dump(h1[:, 5, :], h1_b, 1024)
            finish(); return nc

        A.push()
        specsF = []
        for half_ in range(2):
            f0_ = half_ * 2816
            for fb_ in range(11):
                specsF += [(wsrc("w_g", f0_ + fb_ * 256, 256), 16, 256), (wsrc("w_u", f0_ + fb_ * 256, 256), 16, 256)]
            wdv_ = d["w_d"][f0_:f0_ + 2816, :].rearrange("(k p) c -> p k c", p=128)
            specsF += [(wdv_[:, :, o_ * 128:(o_ + 1) * 128], 22, 128) for o_ in range(16)]
        make_ring(5, specsF)
        hfT, hfT_b = A.alloc("hfT", [16, 1024], BF16)
        actT, actT_b = A.alloc("actT", [22, 1024], BF16)
        sg = [A.alloc("sg%d" % i, [512], F32) for i in range(2)]
        for t in range(2):
            rmsnorm_T(h1[:, :, t * 512:(t + 1) * 512], h1_b, 512, lambda c: nw[:, 2, c:c + 1],
                      lambda c, t=t: hfT[:, c, t * 512:(t + 1) * 512], hfT_b, t)
        si = 0
        for half in range(2):
            f0 = half * 2816
            for fb in range(11):
                fidx = half * 38 + 2 * fb
                wg, wg_b = wget(fidx)
                wu, wu_b = wget(fidx + 1)
                for j in range(2):
                    cc = fb * 2 + j
                    for t in range(2):
                        bg, bu = (0, 1) if (si % 2 == 0) else (2, 3)
                        for kc in range(16):
                            mm(ps[bg], wg[:, kc, j * 128:(j + 1) * 128], hfT[:, kc, t * 512:(t + 1) * 512],
                               kc == 0, kc == 15, [wg_b, hfT_b], [psb[bg]])
                        for kc in range(16):
                            mm(ps[bu], wu[:, kc, j * 128:(j + 1) * 128], hfT[:, kc, t * 512:(t + 1) * 512],
                               kc == 0, kc == 15, [wu_b, hfT_b], [psb[bu]])
                        s_ap, s_b = sg[si % 2]
                        si += 1
                        act(s_ap, ps[bg], AF.Silu, [psb[bg]], [s_b])
                        tt(actT[:, cc, t * 512:(t + 1) * 512], s_ap, ps[bu], ALU.mult, [s_b, psb[bu]], [actT_b])
                wrel(fidx); wrel(fidx + 1)
            for o in range(16):
                wd, wd_b = wget(half * 38 + 22 + o)
                for t in range(2):
                    bank = 4 + (2 * o + t) % 4
                    for kc in range(22):
                        mm(ps[bank], wd[:, kc, :], actT[:, kc, t * 512:(t + 1) * 512], kc == 0, kc == 21,
                           [wd_b, actT_b], [psb[bank]])
                    tt(h1[:, o, t * 512:(t + 1) * 512], h1[:, o, t * 512:(t + 1) * 512], ps[bank], ALU.add,
                       [h1_b, psb[bank]], [h1_b])
                wrel(half * 38 + 22 + o)
        outv = outT.rearrange("(c p) t -> p c t", p=128)
        for q in range(4):
            dma("sp", outv[:, 4 * q:4 * q + 4, :], h1[:, 4 * q:4 * q + 4, :], "dma%d" % (10 + q), reads=[h1_b])
        A.pop()
        finish()
        print("[kernel] ops=%d arena_peak_words=%d of %d" % (P.n_ops, A.peak, A.words))
    return nc


_BF = ml_dtypes.bfloat16


def _consts():
    ident = np.eye(128, dtype=np.float32)
    ones = np.ones((128, 128), np.float32)
    blk = np.zeros((128, 128), np.float32)
    blk[:64, :64] = 1.0
    blk[64:, 64:] = 1.0
    triU = np.triu(np.ones((128, 128), np.float32))
    SL = np.tril(np.ones((128, 128), np.float32), -1)
    c_bf = np.stack([ident, ones, blk, triU, SL], axis=1).astype(_BF)
    c_f32 = np.ascontiguousarray(np.stack([triU, SL, ones], axis=1))
    return np.ascontiguousarray(c_bf), c_f32


def _chunked(v):
    return np.ascontiguousarray(np.asarray(v, np.float32).reshape(-1, 128).T)


def make_in_maps(inp):
    f = lambda k: np.asarray(inp[k], np.float32)
    x = f("x"); mem = f("mem")
    c_bf, c_f32 = _consts()
    nw = np.stack([_chunked(f("mix_norm_w")[0]), _chunked(f("xattn_norm_w")[0]), _chunked(f("ffn_norm_w")[0]),
                   _chunked(f("mem_norm_w")[0]), _chunked(f("ssd_norm_w")[0])], axis=1)
    pp = np.zeros((128, 8), np.float32)
    pp[:, 0] = np.tile(f("da_q_norm_w")[0], 2)
    pp[:, 1] = np.tile(f("da_k_norm_w")[0], 2)
    pp[:, 2] = f("da_subln_w")[0]
    pp[:, 3] = f("xattn_q_norm_w")[0]
    pp[:, 4] = f("xattn_k_norm_w")[0]
    rows = np.concatenate([f("da_lambda_q1")[0], f("da_lambda_k1")[0], f("da_lambda_q2")[0], f("da_lambda_k2")[0],
                           f("ssd_dt_bias")[0], f("ssd_a_log")[0], f("ssd_d")[0]])[None, :]
    cw = f("ssd_conv_w")[0]
    convw = np.ascontiguousarray(cw.reshape(4, 24, 128).transpose(2, 1, 0))
    convb = _chunked(f("ssd_conv_b")[0])
    shared = {
        "w_in": np.ascontiguousarray(f("w_in")[0]), "w_out": np.ascontiguousarray(f("w_out")[0]),
        "w_xq": np.ascontiguousarray(f("xattn_w_q")[0]), "w_xkv": np.ascontiguousarray(f("xattn_w_kv")[0]),
        "w_xo": np.ascontiguousarray(f("xattn_w_o")[0]),
        "w_g": np.ascontiguousarray(f("ffn_w_gate")[0]), "w_u": np.ascontiguousarray(f("ffn_w_up")[0]),
        "w_d": np.ascontiguousarray(f("ffn_w_down")[0]),
        "nw": np.ascontiguousarray(nw), "pp": pp, "rows": np.ascontiguousarray(rows.astype(np.float32)),
        "convw": convw, "convb": convb, "convb_row": np.ascontiguousarray(f("ssd_conv_b")[0][None, :]),
        "c_bf": c_bf, "c_f32": c_f32,
    }
    maps = []
    for core in range(8):
        b, half = core // 2, core % 2
        xT = np.zeros((2048, 2048), np.float32)
        if half == 0:
            xT[:, 1024:] = x[b, 0:1024].T
        else:
            xT[:, :1024] = x[b, 0:1024].T
            xT[:, 1024:] = x[b, 1024:2048].T
        m = dict(shared)
        m["xT"] = xT
        m["memT"] = np.ascontiguousarray(mem[b].T)
        m["flag"] = np.full((128, 1), float(half), np.float32)
        maps.append(m)
    return maps


_NC_CACHE = {}


def kernel(**inputs):
    if "nc" not in _NC_CACHE:
        _NC_CACHE["nc"] = build_program()
    nc = _NC_CACHE["nc"]
    maps = make_in_maps(inputs)
    res = run_bass_kernel_spmd(nc, maps, core_ids=list(range(8)))
    out = np.empty((4, 2048, 2048), np.float32)
    for core in range(8):
        b, half = core // 2, core % 2
        out[b, half * 1024:(half + 1) * 1024, :] = res.results[core]["outT"].T
    return out
```

```python
import numpy as np
import ml_dtypes
from contextlib import ExitStack
import concourse.bass as bass
import concourse.mybir as mybir
from concourse.bass_utils import run_bass_kernel_spmd

F32 = mybir.dt.float32
BF16 = mybir.dt.bfloat16
ALU = mybir.AluOpType
AF = mybir.ActivationFunctionType
EPS = 1e-6

SAME_ENGINE_SYNC = True
RELAX_SAME_ENGINE_WAW = True


class Buf:
    __slots__ = ("name", "w", "r", "excl")

    def __init__(self, name, excl=False):
        self.name = name
        self.w = None
        self.r = []
        self.excl = excl


class Prog:
    ENGS = ("pe", "act", "dve", "pool", "sp")

    def __init__(self, nc, stack, n_dma_sems=14):
        self.nc = nc
        self.sems = {}
        self.count = {}
        self.known = {e: {} for e in self.ENGS}
        self.ops = {e: [] for e in self.ENGS}
        for e in self.ENGS[:4]:
            self.sems[e] = stack.enter_context(nc.semaphore("s_" + e))
            self.count[e] = 0
        for i in range(n_dma_sems):
            k = "dma%d" % i
            self.sems[k] = stack.enter_context(nc.semaphore("s_" + k))
            self.count[k] = 0
        self.n_ops = 0

    def _need(self, eng, dep, waits):
        if dep is None:
            return
        k, v = dep
        if k == eng and (eng == "pe" or not SAME_ENGINE_SYNC):
            return
        if self.known[eng].get(k, 0) >= v:
            return
        if v > waits.get(k, 0):
            waits[k] = v

    def _deps(self, eng, reads, writes):
        waits = {}
        for b in reads:
            self._need(eng, b.w, waits)
            if b.excl:
                for r in b.r:
                    if r[0] != eng:
                        self._need(eng, r, waits)
        for b in writes:
            if not (b.w is not None and b.w[0] == eng and RELAX_SAME_ENGINE_WAW):
                self._need(eng, b.w, waits)
            for r in b.r:
                if r[0] == eng and RELAX_SAME_ENGINE_WAW:
                    continue
                self._need(eng, r, waits)
        for k, v in waits.items():
            self.known[eng][k] = v
        return list(waits.items())

    def _mark(self, tok, reads, writes):
        for b in reads:
            b.r.append(tok)
            if len(b.r) > 48:
                mx = {}
                for k, v in b.r:
                    if v > mx.get(k, 0):
                        mx[k] = v
                b.r = list(mx.items())
        for b in writes:
            b.w = tok
            b.r = []

    def op(self, eng, fn, reads=(), writes=()):
        waits = self._deps(eng, reads, writes)
        self.count[eng] += 1
        tok = (eng, self.count[eng])
        self.ops[eng].append((waits, fn, eng, 1))
        self._mark(tok, reads, writes)
        self.n_ops += 1
        return tok

    def dma(self, queue, fn, semkey, reads=(), writes=()):
        waits = self._deps(queue, reads, writes)
        self.count[semkey] += 16
        tok = (semkey, self.count[semkey])
        self.ops[queue].append((waits, fn, semkey, 16))
        self._mark(tok, reads, writes)
        self.n_ops += 1
        return tok

    def barrier(self):
        for e in self.ENGS:
            self.wait_all(e)

    def wait_all(self, eng):
        waits = []
        for k in self.sems:
            v = self.count[k]
            if k == eng and eng == "pe":
                continue
            if v > self.known[eng].get(k, 0):
                waits.append((k, v))
                self.known[eng][k] = v
        if waits:
            self.ops[eng].append((waits, None, None, 0))

    def emit(self):
        nc = self.nc
        P = self
        with nc.Block() as block:
            def run(ename):
                def body(e):
                    for waits, fn, semkey, inc in P.ops[ename]:
                        for k, v in waits:
                            e.wait_ge(P.sems[k], v)
                        if fn is not None:
                            fn(e).then_inc(P.sems[semkey], inc)
                return body
            block.tensor(run("pe"))
            block.scalar(run("act"))
            block.vector(run("dve"))
            block.gpsimd(run("pool"))
            block.sync(run("sp"))


class Arena:
    def __init__(self, big_ap, words):
        self.big = big_ap
        self.words = words
        self.off = 0
        self.marks = []
        self.peak = 0

    def alloc(self, name, free_shape, dtype, parts=128):
        n = int(np.prod(free_shape))
        esz = 4 if dtype == F32 else 2
        w = (n * esz + 3) // 4
        w = (w + 7) // 8 * 8
        assert self.off + w <= self.words, "SBUF arena overflow at %s: %d+%d>%d" % (
            name, self.off, w, self.words)
        ap = self.big[0:parts, self.off:self.off + w]
        self.off += w
        self.peak = max(self.peak, self.off)
        if dtype != F32:
            ap = ap.bitcast(dtype)
        ap = ap[:, 0:n]
        if len(free_shape) == 2:
            ap = ap.rearrange("p (a b) -> p a b", a=free_shape[0])
        elif len(free_shape) == 3:
            ap = ap.rearrange("p (a b c) -> p a b c", a=free_shape[0], b=free_shape[1])
        return ap, Buf(name)

    def push(self):
        self.marks.append(self.off)

    def pop(self):
        self.off = self.marks.pop()


ARENA_WORDS = 51 * 1024


def build_program(dbg=None):
    nc = bass.Bass("TRN2", target_bir_lowering=False)
    d = {}

    def din(name, shape, dt=F32):
        d[name] = nc.dram_tensor(name, list(shape), dt, kind="ExternalInput").ap()

    din("xT", [2048, 2048]); din("memT", [2048, 256]); din("flag", [128, 1])
    din("w_in", [2048, 8224]); din("w_out", [3072, 2048])
    din("w_xq", [2048, 512]); din("w_xkv", [2048, 1024]); din("w_xo", [512, 2048])
    din("w_g", [2048, 5632]); din("w_u", [2048, 5632]); din("w_d", [5632, 2048])
    din("nw", [128, 5, 16])
    din("pp", [128, 8])
    din("rows", [1, 352])
    din("convw", [128, 24, 4]); din("convb", [128, 24]); din("convb_row", [1, 3072])
    din("c_bf", [128, 5, 128], BF16)
    din("c_f32", [128, 3, 128])
    outT = nc.dram_tensor("outT", [2048, 1024], F32, kind="ExternalOutput").ap()
    dbg_out = None
    if dbg is not None:
        dbg_out = nc.dram_tensor("dbg", [128, 4096], F32, kind="ExternalOutput").ap()

    with ExitStack() as st:
        big = st.enter_context(nc.sbuf_tensor("big", [128, ARENA_WORDS], F32))
        ps_t = [st.enter_context(nc.psum_tensor("ps%d" % i, [128, 512], F32)) for i in range(8)]
        ps = [t[:, :] for t in ps_t]
        psb = [Buf("ps%d" % i, excl=True) for i in range(8)]
        P = Prog(nc, st)
        A = Arena(big, ARENA_WORDS)

        def mm(out, lhsT, rhs, start, stop, reads, writes):
            P.op("pe", lambda e: e.matmul(out, lhsT=lhsT, rhs=rhs, start=start, stop=stop),
                 reads=reads, writes=writes)

        def act(out, in_, func, reads, writes, scale=1.0, bias=0.0, accum_out=None):
            if accum_out is None:
                P.op("act", lambda e: e.activation(out=out, in_=in_, func=func, bias=bias, scale=scale),
                     reads=reads, writes=writes)
            else:
                P.op("act", lambda e: e.activation(out=out, in_=in_, func=func, bias=bias, scale=scale,
                                                   accum_out=accum_out), reads=reads, writes=writes)

        def tt(out, in0, in1, op, reads, writes, eng="dve"):
            P.op(eng, lambda e: e.tensor_tensor(out=out, in0=in0, in1=in1, op=op), reads=reads, writes=writes)

        def stt(out, in0, scalar, in1, op0, op1, reads, writes):
            P.op("dve", lambda e: e.scalar_tensor_tensor(out=out, in0=in0, scalar=scalar, in1=in1, op0=op0, op1=op1),
                 reads=reads, writes=writes)

        def ts(out, in0, s1, op0, reads, writes, s2=None, op1=None, eng="dve"):
            if op1 is None:
                P.op(eng, lambda e: e.tensor_scalar(out=out, in0=in0, scalar1=s1, scalar2=None, op0=op0),
                     reads=reads, writes=writes)
            else:
                P.op(eng, lambda e: e.tensor_scalar(out=out, in0=in0, scalar1=s1, scalar2=s2, op0=op0, op1=op1),
                     reads=reads, writes=writes)

        def cp(out, in_, reads, writes, eng="dve"):
            P.op(eng, lambda e: e.tensor_copy(out=out, in_=in_), reads=reads, writes=writes)

        def memset(ap, val, writes, eng="dve"):
            P.op(eng, lambda e: e.memset(ap, val), writes=writes)

        def dma(queue, out, in_, key, reads=(), writes=()):
            P.dma(queue, lambda e: e.dma_start(out=out, in_=in_), key, reads=reads, writes=writes)

        dbg_state = {"done": False}
        hn_off = 0

        def dump(ap2d, buf, ncols):
            if A.off + ncols > A.words:
                P.barrier()
                A.off = hn_off
            stage, sb = A.alloc("dbgstage", [ncols], F32)
            cp(stage, ap2d, [buf], [sb])
            dma("sp", dbg_out[:, 0:ncols], stage, "dma13", reads=[sb])
            dbg_state["done"] = True

        def finish():
            P.wait_all("sp")
            P.emit()

        cbf, cbf_b = A.alloc("cbf", [5, 128], BF16)
        cf, cf_b = A.alloc("cf", [3, 128], F32)
        nw, nw_b = A.alloc("nw", [5, 16], F32)
        pp, pp_b = A.alloc("pp", [8], F32)
        rows, rows_b = A.alloc("rows", [352], F32)
        convw, convw_b = A.alloc("convw", [24, 4], F32)
        convb, convb_b = A.alloc("convb", [24], F32)
        flag, flag_b = A.alloc("flag", [1], F32)
        sm, sm_b = A.alloc("sm", [16], F32)
        onesflag, onesflag_b = A.alloc("onesflag", [128], BF16)
        consts = [(cbf, d["c_bf"], cbf_b), (cf, d["c_f32"], cf_b), (nw, d["nw"], nw_b), (pp, d["pp"], pp_b),
                  (rows, d["rows"].partition_broadcast(128), rows_b), (convw, d["convw"], convw_b),
                  (convb, d["convb"], convb_b), (flag, d["flag"], flag_b)]
        for (dst, src, b) in consts:
            dma("sp", dst, src, "dma0", writes=[b])
        fin = ("dma0", P.count["dma0"])
        for (_, _, b) in consts:
            b.w = fin
        ident = cbf[:, 0, :]; ones_bf = cbf[:, 1, :]; blockones = cbf[:, 2, :]; tri_bf = cbf[:, 3, :]; SL_bf = cbf[:, 4, :]
        triU = cf[:, 0, :]; SLm = cf[:, 1, :]; ones_f = cf[:, 2, :]
        ts(sm[:, 0:1], pp[:, 0:1], 0.125, ALU.mult, [pp_b], [sm_b])
        cp(sm[:, 1:2], pp[:, 1:2], [pp_b], [sm_b])
        ts(sm[:, 2:3], pp[:, 2:3], 0.8, ALU.mult, [pp_b], [sm_b])
        ts(sm[:, 3:4], pp[:, 3:4], float(128 ** -0.5), ALU.mult, [pp_b], [sm_b])
        cp(sm[:, 4:5], pp[:, 4:5], [pp_b], [sm_b])
        junk, junk_b = A.alloc("junk64", [64], F32)
        stt(junk, rows[:, 0:64], 1.0, rows[:, 64:128], ALU.mult, ALU.mult, [rows_b], [junk_b])
        P.op("dve", lambda e: e.tensor_reduce(out=sm[:, 6:7], in_=junk, axis=mybir.AxisListType.X, op=ALU.add),
             reads=[junk_b], writes=[sm_b])
        stt(junk, rows[:, 128:192], 1.0, rows[:, 192:256], ALU.mult, ALU.mult, [rows_b, sm_b], [junk_b])
        P.op("dve", lambda e: e.tensor_reduce(out=sm[:, 7:8], in_=junk, axis=mybir.AxisListType.X, op=ALU.add),
             reads=[junk_b], writes=[sm_b])
        act(sm[:, 6:8], sm[:, 6:8], AF.Exp, [sm_b], [sm_b])
        tt(sm[:, 5:6], sm[:, 6:7], sm[:, 7:8], ALU.subtract, [sm_b], [sm_b])
        ts(sm[:, 5:6], sm[:, 5:6], 0.2, ALU.add, [sm_b], [sm_b])
        ts(onesflag, ones_bf, flag[:, 0:1], ALU.mult, [cbf_b, flag_b], [onesflag_b])
        abc, abc_b = A.alloc("abc", [32], F32)
        act(abc, rows[:, 288:320], AF.Exp, [rows_b], [abc_b])
        ts(abc, abc, -1.0, ALU.mult, [abc_b], [abc_b])
        dtbias_bc = rows[:, 256:288]
        D_bc = rows[:, 320:352]

        if dbg == "c0":
            dump(sm, sm_b, 16)
            finish(); return nc
        SLOT_ELEMS = 4096
        ring = {"slots": [], "i": 0}

        def make_ring(n, specs):
            ring["slots"] = []
            for i in range(n):
                ap, b = A.alloc("wslot%d" % i, [SLOT_ELEMS], BF16)
                ring["slots"].append((ap, b, "dma%d" % (1 + i)))
            ring["specs"] = specs
            ring["issued"] = 0
            ring["released"] = [False] * len(specs)
            ring["tiles"] = {}
            _pump()

        def _pump():
            n = len(ring["slots"])
            while ring["issued"] < len(ring["specs"]) and (
                    ring["issued"] < n or ring["released"][ring["issued"] - n]):
                i = ring["issued"]
                src_ap, kc, ncols = ring["specs"][i]
                ap, b, key = ring["slots"][i % n]
                assert kc * ncols <= SLOT_ELEMS
                view = ap[:, 0:kc * ncols].rearrange("p (k c) -> p k c", k=kc)
                dma("pool", view, src_ap, key, writes=[b])
                ring["tiles"][i] = (view, b)
                ring["issued"] += 1

        def wget(i):
            _pump()
            assert i in ring["tiles"], "weight tile %d not loadable yet (slot predecessor not released)" % i
            return ring["tiles"][i]

        def wrel(i):
            ring["released"][i] = True
            _pump()

        def wsrc(name, c0, ncols, r0=0, nrows=None):
            t = d[name]
            if nrows is None:
                nrows = t.shape[0]
            return t[r0:r0 + nrows, c0:c0 + ncols].rearrange("(k p) c -> p k c", p=128)

        def rstd_from_ps(ps_ap, ps_buf, out_ap, out_buf, inv_n):
            act(out_ap, ps_ap, AF.Ln, [ps_buf], [out_buf], scale=inv_n, bias=EPS)
            act(out_ap, out_ap, AF.Exp, [out_buf], [out_buf], scale=-0.5)

        sqr = []
        for i in range(2):
            sqr.append(A.alloc("sq%d" % i, [512], BF16))
        rstd_t, rstd_b = A.alloc("rstd", [512], F32)
        sq_i = {"i": 0}

        def rmsnorm_T(src, src_b, n, wcol, dst_fn, dst_b, bank):
            for c in range(16):
                sq, sq_b = sqr[sq_i["i"] % 2]
                sq_i["i"] += 1
                act(sq[:, 0:n], src[:, c, :], AF.Square, [src_b], [sq_b])
                mm(ps[bank][:, 0:n], ones_bf, sq[:, 0:n], c == 0, c == 15, [cbf_b, sq_b], [psb[bank]])
            if dbg == "a1":
                dump(ps[bank][:, 0:n], psb[bank], n)
                return
            rstd_from_ps(ps[bank][:, 0:n], psb[bank], rstd_t[:, 0:n], rstd_b, 1.0 / 2048)
            if dbg == "a2":
                dump(rstd_t[:, 0:n], rstd_b, n)
                return
            for c in range(16):
                stt(dst_fn(c), src[:, c, :], wcol(c), rstd_t[:, 0:n], ALU.mult, ALU.mult,
                    [src_b, nw_b, rstd_b], [dst_b])

        hn_off = A.off
        hnT, hnT_b = A.alloc("hnT", [16, 2048], BF16)
        mix_off = A.off
        mixT, mixT_b = A.alloc("mixT", [24, 1024], BF16)
        A.push()
        xst = []
        for i in range(2):
            ap, b = A.alloc("xst%d" % i, [16, 512], F32)
            xst.append((ap, b, "dma%d" % (7 + i)))
        xTv = d["xT"].rearrange("(c p) t -> p c t", p=128)
        for t4 in range(4):
            xs, xs_b, key = xst[t4 % 2]
            for q in range(4):
                dma("sp", xs[:, 4 * q:4 * q + 4, :], xTv[:, 4 * q:4 * q + 4, t4 * 512:(t4 + 1) * 512], key, writes=[xs_b])
            rmsnorm_T(xs, xs_b, 512, lambda c: nw[:, 0, c:c + 1],
                      lambda c, t4=t4: hnT[:, c, t4 * 512:(t4 + 1) * 512], hnT_b, t4 % 2)
            if dbg in ("a1", "a2"):
                finish(); return nc
            if dbg == "a3":
                dump(hnT[:, 3, 0:512], hnT_b, 512)
                finish(); return nc
        A.pop()
        P.barrier()
        if dbg == "hn":
            dump(hnT[:, 3, 1024:2048], hnT_b, 1024)
            finish(); return nc

        A.push()
        specsB = [(wsrc("w_in", 8192, 32), 16, 32)]
        for g_ in range(4):
            specsB += [(wsrc("w_in", 5120 + 512 * g_, 256), 16, 256), (wsrc("w_in", 5120 + 512 * g_ + 256, 256), 16, 256),
                       (wsrc("w_in", 7168 + 128 * g_, 128), 16, 128), (wsrc("w_in", 7680 + 128 * g_, 128), 16, 128),
                       (wsrc("w_in", 3072 + 512 * g_, 256), 16, 256), (wsrc("w_in", 3072 + 512 * g_ + 256, 256), 16, 256)]
        make_ring(3, specsB)
        dt_all, dt_b = A.alloc("dt_all", [16, 32], F32)
        adt_all, adt_b = A.alloc("adt_all", [16, 32], F32)
        E_all, E_b = A.alloc("E_all", [16, 32], F32)
        dtds_all, dtds_b = A.alloc("dtds_all", [16, 32], F32)
        cd_all, cd_b = A.alloc("cd_all", [16, 32], F32)
        rseg_hi, rseg_b = A.alloc("rseg_hi", [8, 128], BF16)
        rseg_lo, _ = A.alloc("rseg_lo", [8, 128], BF16)
        tmpd, tmpd_b = A.alloc("tmpd", [16, 32], F32)
        adt_hi, adth_b = A.alloc("adt_hi", [16, 32], BF16)
        adt_lo, _ = A.alloc("adt_lo", [16, 32], BF16)
        wdt, wdt_b = wget(0)
        for tb in range(16):
            for kc in range(16):
                mm(ps[0][:, tb * 32:(tb + 1) * 32], hnT[:, kc, tb * 128:(tb + 1) * 128], wdt[:, kc, :],
                   kc == 0, kc == 15, [hnT_b, wdt_b], [psb[0]])
        if dbg == "dt0":
            dump(ps[0], psb[0], 512)
            finish(); return nc
        wrel(0)
        ps0_3 = ps[0].rearrange("p (a b) -> p a b", a=16)
        tt(tmpd, ps0_3, dtbias_bc.unsqueeze(1).to_broadcast([128, 16, 32]), ALU.add, [psb[0], rows_b], [tmpd_b])
        act(tmpd, tmpd, AF.Exp, [tmpd_b], [tmpd_b])
        act(dt_all, tmpd, AF.Ln, [tmpd_b], [dt_b], bias=1.0)
        tt(adt_all, dt_all, abc.unsqueeze(1).to_broadcast([128, 16, 32]), ALU.mult, [dt_b, abc_b], [adt_b])
        if dbg == "dt1":
            dump(dt_all.rearrange("p a b -> p (a b)"), dt_b, 512)
            finish(); return nc
        adt_flat = adt_all.rearrange("p a b -> p (a b)")
        cp(adt_hi, adt_all, [adt_b], [adth_b])
        tt(adt_lo, adt_all, adt_hi, ALU.subtract, [adt_b, adth_b], [adth_b])
        ahf = adt_hi.rearrange("p a b -> p (a b)")
        alf = adt_lo.rearrange("p a b -> p (a b)")
        mm(ps[1], tri_bf, ahf, True, False, [cbf_b, adth_b], [psb[1]])
        mm(ps[1], tri_bf, alf, False, True, [cbf_b, adth_b], [psb[1]])
        mm(ps[2], ones_bf, ahf, True, False, [cbf_b, adth_b], [psb[2]])
        mm(ps[2], ones_bf, alf, False, True, [cbf_b, adth_b], [psb[2]])
        act(E_all.rearrange("p a b -> p (a b)"), ps[1], AF.Exp, [psb[1]], [E_b])
        cp(tmpd.rearrange("p a b -> p (a b)"), ps[1], [psb[1]], [tmpd_b])
        tt(tmpd.rearrange("p a b -> p (a b)"), ps[2], tmpd.rearrange("p a b -> p (a b)"), ALU.subtract,
           [psb[2], tmpd_b], [tmpd_b])
        act(tmpd, tmpd, AF.Exp, [tmpd_b], [tmpd_b])
        tt(dtds_all, dt_all, tmpd, ALU.mult, [dt_b, tmpd_b], [dtds_b])
        act(cd_all.rearrange("p a b -> p (a b)"), ps[2], AF.Exp, [psb[2]], [cd_b])
        if dbg == "dt":
            dump(dtds_all.rearrange("p a b -> p (a b)"), dtds_b, 512)
            finish(); return nc

        diagWs = [A.alloc("diagW%d" % i, [4, 128], BF16) for i in range(2)]
        xpre = [A.alloc("xpre%d" % i, [2051], BF16) for i in range(1)]
        for ap, b in xpre:
            memset(ap[:, 0:3], 0.0, [b])
        x_tok = mixT[:, 0:8, :].rearrange("p a (b c) -> p (a b) c", c=512)
        x_tok_b = Buf("x_tok")
        B_tok, B_tok_b = A.alloc("B_tok", [16, 128], BF16)
        xsT, xsT_b = A.alloc("xsT", [2048], BF16)
        BT, BT_b = A.alloc("BT", [1024], BF16)
        CT, CT_b = A.alloc("CT", [1024], BF16)
        zs, zs_b = A.alloc("zs", [8, 512], BF16)
        S, S_b = A.alloc("S", [8, 64], F32)
        S_bf, S_bf_b = A.alloc("S_bf", [2, 512], BF16)
        xdtds, xdtds_b = A.alloc("xdtds", [8, 64], BF16)
        xdt, xdt_b = A.alloc("xdt", [8, 64], BF16)
        xD, xD_b = A.alloc("xD", [8, 64], BF16)
        CBm, CBm_b = A.alloc("CBm", [128], BF16)
        MT, MT_b = A.alloc("MT", [8, 128], BF16)
        t1, t1_b = A.alloc("t1", [8, 64], F32)
        yn, yn_b = A.alloc("yn", [512], BF16)
        ssv, ssv_b = A.alloc("ssv", [2], F32)

        for g in range(4):
            hs = slice(8 * g, 8 * g + 8)
            chunks = [4 * g, 4 * g + 1, 4 * g + 2, 4 * g + 3, 16 + g, 20 + g]
            wbase = 1 + 6 * g
            for ci in range(5):
                xp, xp_b = xpre[0]
                diagW, diagW_b = diagWs[ci % 2]
                for j in range(4):
                    ts(diagW[:, j, :], ident, convw[:, chunks[ci], j:j + 1], ALU.mult, [cbf_b, convw_b], [diagW_b])
                widx = wbase + (ci // 2 if ci < 4 else ci - 2)
                (w_ap, w_b), c0 = wget(widx), ((ci % 2) * 128 if ci < 4 else 0)
                tts = (1, 2, 3) if ci == 5 else (0, 1, 2, 3)
                for t4 in tts:
                    bank = t4 % 2
                    for kc in range(16):
                        mm(ps[bank], w_ap[:, kc, c0:c0 + 128], hnT[:, kc, t4 * 512:(t4 + 1) * 512],
                           kc == 0, kc == 15, [w_b, hnT_b], [psb[bank]])
                    act(xp[:, 3 + t4 * 512:3 + (t4 + 1) * 512], ps[bank], AF.Copy, [psb[bank]], [xp_b])
                if ci in (1, 3, 4, 5):
                    wrel(widx)
                if ci < 5:
                    for t4 in range(4):
                        bank = 2 + t4 % 2
                        for j in range(4):
                            mm(ps[bank], diagW[:, j, :], xp[:, t4 * 512 + j:t4 * 512 + j + 512],
                               j == 0, j == 3, [diagW_b, xp_b], [psb[bank]])
                        act(xsT[:, t4 * 512:(t4 + 1) * 512], ps[bank], AF.Silu, [psb[bank], convb_b], [xsT_b],
                            bias=convb[:, chunks[ci]:chunks[ci] + 1])
                    dst = x_tok if ci < 4 else B_tok
                    dst_b = x_tok_b if ci < 4 else B_tok_b
                    dc0 = ci * 128 if ci < 4 else 0
                    for tq in range(4):
                        bank = 4 + tq % 2
                        pstx = ps[bank].bitcast(BF16)
                        for i in range(4):
                            tb = tq * 4 + i
                            P.op("pe", lambda e, i=i, tb=tb, pstx=pstx: e.transpose(
                                out=pstx[:, i * 128:(i + 1) * 128], in_=xsT[:, tb * 128:(tb + 1) * 128], identity=ident),
                                reads=[xsT_b, cbf_b], writes=[psb[bank]])
                        cp(dst[:, tq * 4:tq * 4 + 4, dc0:dc0 + 128],
                           pstx[:, 0:512].rearrange("p (a b) -> p a b", a=4), [psb[bank]], [dst_b])
                    if ci == 4:
                        cp(BT, xsT[:, 1024:2048], [xsT_b], [BT_b])
                if ci == 5:
                    for t in range(2):
                        bank = 2 + t
                        for j in range(4):
                            mm(ps[bank], diagW[:, j, :], xp[:, 1024 + t * 512 + j:1024 + t * 512 + j + 512],
                               j == 0, j == 3, [diagW_b, xp_b], [psb[bank]])
                        act(CT[:, t * 512:(t + 1) * 512], ps[bank], AF.Silu, [psb[bank], convb_b], [CT_b],
                            bias=convb[:, chunks[ci]:chunks[ci] + 1])
            def c_step(k, chunks=chunks, wbase=wbase):
                xp, xp_b = xpre[0]
                diagW, diagW_b = diagWs[1]
                widx = wbase + 3
                if k == 0:
                    for j in range(4):
                        ts(diagW[:, j, :], ident, convw[:, chunks[5], j:j + 1], ALU.mult, [cbf_b, convw_b], [diagW_b])
                tlist = {0: (1, 2), 1: (3,), 2: ()}[k]
                if tlist:
                    w_ap, w_b = wget(widx)
                for t4 in tlist:
                    bank = t4 % 2
                    for kc in range(16):
                        mm(ps[bank], w_ap[:, kc, 0:128], hnT[:, kc, t4 * 512:(t4 + 1) * 512],
                           kc == 0, kc == 15, [w_b, hnT_b], [psb[bank]])
                    act(xp[:, 3 + t4 * 512:3 + (t4 + 1) * 512], ps[bank], AF.Copy, [psb[bank]], [xp_b])
                if k == 1:
                    wrel(widx)
                if k == 2:
                    for t in range(2):
                        bank = 2 + t
                        for j in range(4):
                            mm(ps[bank], diagW[:, j, :], xp[:, 1024 + t * 512 + j:1024 + t * 512 + j + 512],
                               j == 0, j == 3, [diagW_b, xp_b], [psb[bank]])
                        act(CT[:, t * 512:(t + 1) * 512], ps[bank], AF.Silu, [psb[bank], convb_b], [CT_b],
                            bias=convb[:, chunks[5]:chunks[5] + 1])

            wzh = [wget(wbase + 4), wget(wbase + 5)]

            def zproj(tb, wzh=wzh):
                bank = 2 if tb % 2 == 0 else 1
                for i in range(2):
                    for kc in range(16):
                        mm(ps[bank][:, 256 * i:256 * i + 256], hnT[:, kc, 1024 + tb * 128:1024 + (tb + 1) * 128],
                           wzh[i][0][:, kc, :], kc == 0, kc == 15, [hnT_b, wzh[i][1]], [psb[bank]])
                act(zs[:, tb, :], ps[bank], AF.Silu, [psb[bank]], [zs_b])
            if dbg == "xtok" and g == 0:
                dump(x_tok[:, 9, :], x_tok_b, 512)
                finish(); return nc
            if dbg == "ct" and g == 0:
                dump(CT, CT_b, 1024)
                finish(); return nc
            memset(S, 0.0, [S_b])
            memset(S_bf, 0.0, [S_bf_b])

            def bc(ap2):
                return ap2.unsqueeze(2).to_broadcast([128, 8, 64])

            def stA(c):
                own = c >= 8
                x3 = x_tok[:, c, :].rearrange("p (h j) -> p h j", h=8)
                if own:
                    tl = (c - 8) * 128
                    tt(rseg_hi, tri_bf.unsqueeze(1).to_broadcast([128, 8, 128]),
                       adt_hi[:, c, hs].unsqueeze(2).to_broadcast([128, 8, 128]), ALU.mult,
                       [cbf_b, adth_b], [rseg_b], eng="pool")
                    tt(rseg_lo, tri_bf.unsqueeze(1).to_broadcast([128, 8, 128]),
                       adt_lo[:, c, hs].unsqueeze(2).to_broadcast([128, 8, 128]), ALU.mult,
                       [cbf_b, adth_b], [rseg_b], eng="pool")
                if c < 15:
                    tt(xdtds, x3, bc(dtds_all[:, c, hs]), ALU.mult, [x_tok_b, dtds_b], [xdtds_b])
                if own:
                    tt(xdt, x3, bc(dt_all[:, c, hs]), ALU.mult, [x_tok_b, dt_b], [xdt_b], eng="pool")
                    tt(xD, x3, bc(D_bc[:, hs]), ALU.mult, [x_tok_b, rows_b], [xD_b])
                    mm(ps[6][:, 0:128], BT[:, tl:tl + 128], CT[:, tl:tl + 128], True, True,
                       [BT_b, CT_b], [psb[6]])
                    for hh in range(2):
                        mm(ps[hh], SL_bf, rseg_hi[:, 4 * hh:4 * hh + 4, :].rearrange("p a b -> p (a b)"), True, False,
                           [cbf_b, rseg_b], [psb[hh]])
                        mm(ps[hh], SL_bf, rseg_lo[:, 4 * hh:4 * hh + 4, :].rearrange("p a b -> p (a b)"), False, True,
                           [cbf_b, rseg_b], [psb[hh]])
                        act(MT[:, 4 * hh:4 * hh + 4, :].rearrange("p a b -> p (a b)"), ps[hh], AF.Exp,
                            [psb[hh]], [MT_b])
                if c < 15:
                    mm(ps[4], B_tok[:, c, :], xdtds.rearrange("p h j -> p (h j)"), True, True,
                       [B_tok_b, xdtds_b], [psb[4]])

            def stB(c):
                own = c >= 8
                if own:
                    tl = (c - 8) * 128
                    tt(CBm, ps[6][:, 0:128], triU, ALU.mult, [psb[6], cf_b], [CBm_b])
                    tt(MT, MT, CBm.unsqueeze(1).to_broadcast([128, 8, 128]), ALU.mult, [MT_b, CBm_b], [MT_b])
                    mm(ps[7], ident, xD.rearrange("p h j -> p (h j)"), True, False, [cbf_b, xD_b], [psb[7]])
                    for h in range(8):
                        mm(ps[7][:, h * 64:(h + 1) * 64], MT[:, h, :], xdt[:, h, :], False, h == 7,
                           [MT_b, xdt_b], [psb[7]])
                    mm(ps[5], CT[:, tl:tl + 128], S_bf[:, c % 2, :], True, True, [CT_b, S_bf_b], [psb[5]])
                if c < 15:
                    tt(S, S, bc(cd_all[:, c, hs]), ALU.mult, [S_b, cd_b], [S_b], eng="pool")
                    Sf = S.rearrange("p h j -> p (h j)")
                    tt(Sf, Sf, ps[4], ALU.add, [S_b, psb[4]], [S_b])
                    if c == 7:
                        ts(Sf, Sf, flag[:, 0:1], ALU.mult, [S_b, flag_b], [S_b])
                    if c >= 7:
                        act(S_bf[:, (c + 1) % 2, :], Sf, AF.Copy, [S_b], [S_bf_b])

            def stT1(c):
                if c < 8:
                    return
                tt(t1, ps[5].rearrange("p (h j) -> p h j", h=8), bc(E_all[:, c, hs]), ALU.mult,
                   [psb[5], E_b], [t1_b])
                t1f = t1.rearrange("p h j -> p (h j)")
                tt(t1f, t1f, ps[7], ALU.add, [t1_b, psb[7]], [t1_b])
                tt(t1f, t1f, zs[:, c - 8, :], ALU.mult, [t1_b, zs_b], [t1_b])
                act(yn, t1f, AF.Square, [t1_b], [yn_b, ssv_b], accum_out=ssv[:, 0:1])
                act(ssv[:, 1:2], ssv[:, 0:1], AF.Ln, [ssv_b], [ssv_b], scale=1.0 / 512, bias=EPS)
                act(ssv[:, 1:2], ssv[:, 1:2], AF.Exp, [ssv_b], [ssv_b], scale=-0.5)

            def stT2(c):
                if c < 8:
                    return
                tl = (c - 8) * 128
                t1f = t1.rearrange("p h j -> p (h j)")
                act(yn, t1f, AF.Copy, [t1_b, ssv_b], [yn_b], scale=ssv[:, 1:2])
                pst = ps[3].bitcast(BF16)
                for j in range(4):
                    P.op("pe", lambda e, j=j, pst=pst: e.transpose(out=pst[:, j * 128:(j + 1) * 128],
                                                                 in_=yn[:, j * 128:(j + 1) * 128], identity=ident),
                         reads=[yn_b, cbf_b], writes=[psb[3]])
                tt(mixT[:, 8 + 4 * g:12 + 4 * g, tl:tl + 128],
                   pst[:, 0:512].rearrange("p (a b) -> p a b", a=4),
                   nw[:, 4, 4 * g:4 * g + 4].unsqueeze(2).to_broadcast([128, 4, 128]), ALU.mult,
                   [psb[3], nw_b], [mixT_b])

            stA(0)
            stB(0)
            for c in range(16):
                if c < 3:
                    c_step(c)
                if c < 8:
                    zproj(c)
                    if c == 7:
                        wrel(wbase + 4); wrel(wbase + 5)
                if c + 1 <= 15:
                    stA(c + 1)
                stT1(c)
                if c + 1 <= 15:
                    stB(c + 1)
                stT2(c)
            if dbg == "ssd" and g == 0:
                dump(mixT[:, 8, :], mixT_b, 1024)
                finish(); return nc
        A.pop()
        P.barrier()

        A.push()
        specsC = []
        for h_ in range(8):
            specsC += [(wsrc("w_in", h_ * 128, 128), 16, 128), (wsrc("w_in", 1024 + h_ * 128, 128), 16, 128),
                       (wsrc("w_in", 2048 + h_ * 128, 128), 16, 128)]
        make_ring(5, specsC)
        qT, qT_b = A.alloc("qT", [1024], BF16)
        kT, kT_b = A.alloc("kT", [2048], BF16)
        Vh, Vh_b = A.alloc("Vh", [16, 128], BF16)
        Pb = [[A.alloc("P%d_%d" % (br, i), [512], BF16) for i in range(2)] for br in range(2)]
        R1, R1_b = A.alloc("R1", [512], F32)
        R2, R2_b = A.alloc("R2", [512], F32)
        ob, ob_b = A.alloc("ob", [512], F32)
        ob2, ob2_b = A.alloc("ob2", [512], F32)
        lnq, lnq_b = A.alloc("lnq", [512], F32)
        sqa, sqa_b = A.alloc("sqa", [512], BF16)
        vstage = [A.alloc("vst%d" % i, [512], BF16) for i in range(2)]
        sqn = [(sqa, sqa_b), A.alloc("sqn1", [512], BF16)]
        lnn = [(lnq, lnq_b), A.alloc("lnn1", [512], F32)]
        obfin = [A.alloc("obf%d" % i, [512], F32) for i in range(2)]
        sqf = [A.alloc("sqf%d" % i, [512], BF16) for i in range(2)]
        pending_f2 = []

        def flush_f2():
            if not pending_f2:
                return
            hh_, slot_ = pending_f2.pop(0)
            obf, obf_b = obfin[slot_]
            mm(ps[0], ones_bf, sqf[slot_][0], True, True, [cbf_b, sqf[slot_][1]], [psb[0]])
            rstd_from_ps(ps[0], psb[0], lnq, lnq_b, 1.0 / 128)
            stt(mixT[:, hh_, slot_ * 512:(slot_ + 1) * 512], obf, sm[:, 2:3], lnq, ALU.mult, ALU.mult,
                [obf_b, sm_b, lnq_b], [mixT_b])

        def qknorm(psrc, psrc_b, pss, pss_b, wcol, dst, dst_b, blk, inv_n):
            act(sqa, psrc, AF.Square, [psrc_b], [sqa_b])
            mm(pss, blk, sqa, True, True, [cbf_b, sqa_b], [pss_b])
            rstd_from_ps(pss, pss_b, lnq, lnq_b, inv_n)
            stt(dst, psrc, wcol, lnq, ALU.mult, ALU.mult, [psrc_b, sm_b, lnq_b], [dst_b])

        for h in range(8):
            wq, wq_b = wget(3 * h)
            wk, wk_b = wget(3 * h + 1)
            wv, wv_b = wget(3 * h + 2)
            ntl = [(wq, wq_b, 1024 + t * 512, sm[:, 0:1], qT[:, t * 512:(t + 1) * 512], qT_b) for t in range(2)]
            ntl += [(wk, wk_b, t4 * 512, sm[:, 1:2], kT[:, t4 * 512:(t4 + 1) * 512], kT_b) for t4 in range(4)]

            def nproj(i):
                w_, w_b_, tok0, _, _, _ = ntl[i]
                a = 2 * (i % 2)
                for kc in range(16):
                    mm(ps[a], w_[:, kc, :], hnT[:, kc, tok0:tok0 + 512], kc == 0, kc == 15, [w_b_, hnT_b], [psb[a]])
                sq_, sq_b_ = sqn[i % 2]
                act(sq_, ps[a], AF.Square, [psb[a]], [sq_b_])

            def nfin(i):
                _, _, _, wcol, dst, dst_b = ntl[i]
                a, b2 = 2 * (i % 2), 2 * (i % 2) + 1
                sq_, sq_b_ = sqn[i % 2]
                ln_, ln_b_ = lnn[i % 2]
                mm(ps[b2], blockones, sq_, True, True, [cbf_b, sq_b_], [psb[b2]])
                rstd_from_ps(ps[b2], psb[b2], ln_, ln_b_, 1.0 / 64)
                stt(dst, ps[a], wcol, ln_, ALU.mult, ALU.mult, [psb[a], sm_b, ln_b_], [dst_b])

            nproj(0)
            for i in range(6):
                if i + 1 < 6:
                    nproj(i + 1)
                nfin(i)
            for t4 in range(4):
                bank = 4 + t4 % 2
                for kc in range(16):
                    mm(ps[bank], wv[:, kc, :], hnT[:, kc, t4 * 512:(t4 + 1) * 512], kc == 0, kc == 15,
                       [wv_b, hnT_b], [psb[bank]])
                vst, vst_b = vstage[t4 % 2]
                act(vst, ps[bank], AF.Copy, [psb[bank]], [vst_b])
                tbank = 6 + t4 % 2
                pstv = ps[tbank].bitcast(BF16)
                for i in range(4):
                    P.op("pe", lambda e, i=i, pstv=pstv, vst=vst: e.transpose(
                        out=pstv[:, i * 128:(i + 1) * 128], in_=vst[:, i * 128:(i + 1) * 128], identity=ident),
                        reads=[vst_b, cbf_b], writes=[psb[tbank]])
                src3 = pstv[:, 0:512].rearrange("p (a b) -> p a b", a=4)
                if t4 < 2:
                    ts(Vh[:, t4 * 4:t4 * 4 + 4, :], src3, flag[:, 0:1], ALU.mult, [psb[tbank], flag_b], [Vh_b])
                else:
                    cp(Vh[:, t4 * 4:t4 * 4 + 4, :], src3, [psb[tbank]], [Vh_b])
            wrel(3 * h); wrel(3 * h + 1); wrel(3 * h + 2)
            if dbg == "qk" and h == 0:
                dump(kT, kT_b, 2048)
                finish(); return nc
            for slot in range(2):
                nkb = 8 + 4 * (slot + 1)
                def geom(kb):
                    j = kb - 8 - 4 * slot
                    q0 = j * 128 if j >= 0 else 0
                    return j, q0, kb % 2

                def emit_S(kb):
                    j, q0, bsel = geom(kb)
                    S1, S2 = ps[2 * bsel], ps[2 * bsel + 1]
                    S1b, S2b = psb[2 * bsel], psb[2 * bsel + 1]
                    qs = slice(slot * 512 + q0, slot * 512 + 512)
                    mm(S1[:, q0:512], kT[0:64, kb * 128:(kb + 1) * 128], qT[0:64, qs], True, True, [kT_b, qT_b], [S1b])
                    mm(S2[:, q0:512], kT[64:128, kb * 128:(kb + 1) * 128], qT[64:128, qs], True, True, [kT_b, qT_b], [S2b])
                    (P1, P1b), (P2, P2b) = Pb[0][bsel], Pb[1][bsel]
                    act(P1[:, q0:512], S1[:, q0:512], AF.Exp, [S1b], [P1b])
                    act(P2[:, q0:512], S2[:, q0:512], AF.Exp, [S2b], [P2b])
                    if j >= 0:
                        tt(P1[:, q0:q0 + 128], P1[:, q0:q0 + 128], tri_bf, ALU.mult, [P1b, cbf_b], [P1b])
                        tt(P2[:, q0:q0 + 128], P2[:, q0:q0 + 128], tri_bf, ALU.mult, [P2b, cbf_b], [P2b])

                def emit_PV(kb):
                    j, q0, bsel = geom(kb)
                    (P1, P1b), (P2, P2b) = Pb[0][bsel], Pb[1][bsel]
                    on, on_b = (onesflag, onesflag_b) if kb < 8 else (ones_bf, cbf_b)
                    first, last = kb == 0, kb == nkb - 1
                    mm(ps[4][:, q0:512], Vh[:, kb, :], P1[:, q0:512], first, last, [Vh_b, P1b], [psb[4]])
                    mm(ps[6][:, q0:512], on, P1[:, q0:512], first, last, [on_b, P1b], [psb[6]])
                    mm(ps[5][:, q0:512], Vh[:, kb, :], P2[:, q0:512], first, last, [Vh_b, P2b], [psb[5]])
                    mm(ps[7][:, q0:512], on, P2[:, q0:512], first, last, [on_b, P2b], [psb[7]])

                emit_S(0)
                for kb in range(nkb):
                    if kb + 1 < nkb:
                        emit_S(kb + 1)
                    if kb == 1:
                        flush_f2()
                    emit_PV(kb)
                pending_f2.append((h, slot))
                act(R1, ps[6], AF.Ln, [psb[6]], [R1_b])
                act(R1, R1, AF.Exp, [R1_b], [R1_b], scale=-1.0)
                act(R2, ps[7], AF.Ln, [psb[7]], [R2_b])
                act(R2, R2, AF.Exp, [R2_b], [R2_b], scale=-1.0)
                tt(ob, ps[4], R1, ALU.mult, [psb[4], R1_b], [ob_b])
                stt(ob2, ps[5], sm[:, 5:6], R2, ALU.mult, ALU.mult, [psb[5], sm_b, R2_b], [ob2_b])
                obf, obf_b = obfin[slot]
                tt(obf, ob, ob2, ALU.subtract, [ob_b, ob2_b], [obf_b])
                act(sqf[slot][0], obf, AF.Square, [obf_b], [sqf[slot][1]])
            if h == 7:
                flush_f2()
            if dbg == "attn" and h == 0:
                flush_f2()
                dump(mixT[:, 0, :], mixT_b, 1024)
                finish(); return nc
        A.pop()
        P.barrier()
        h1 = big[:, hn_off:hn_off + 16384].rearrange("p (a b) -> p a b", a=16)
        h1_b = Buf("h1")

        A.push()
        woutv = d["w_out"].rearrange("(k p) c -> p k c", p=128)
        make_ring(6, [(woutv[:, :, o_ * 128:(o_ + 1) * 128], 24, 128) for o_ in range(16)])
        xres = [A.alloc("xres%d" % i, [512], F32) for i in range(2)]
        xi = 0
        woutv = d["w_out"].rearrange("(k p) c -> p k c", p=128)
        for o in range(16):
            wo_t, wo_b = wget(o)
            for t in range(2):
                bank = (2 * o + t) % 4
                xr, xr_b = xres[xi % 2]
                key = "dma%d" % (7 + xi % 2)
                xi += 1
                dma("sp", xr, d["xT"][o * 128:(o + 1) * 128, 1024 + t * 512:1024 + (t + 1) * 512], key, writes=[xr_b])
                for c in range(24):
                    mm(ps[bank], wo_t[:, c, :], mixT[:, c, t * 512:(t + 1) * 512], c == 0, c == 23,
                       [wo_b, mixT_b], [psb[bank]])
                tt(h1[:, o, t * 512:(t + 1) * 512], ps[bank], xr, ALU.add, [psb[bank], xr_b], [h1_b])
            wrel(o)
        A.pop()
        P.barrier()
        if dbg == "h1":
            dump(h1[:, 5, :], h1_b, 1024)
            finish(); return nc

        A.off = mix_off

        A.push()
        woxv = d["w_xo"].rearrange("(k p) c -> p k c", p=128)
        specsE = [(wsrc("w_xkv", 0, 256), 16, 256), (wsrc("w_xkv", 256, 256), 16, 256),
                  (wsrc("w_xkv", 512, 256), 16, 256), (wsrc("w_xkv", 768, 256), 16, 256),
                  (wsrc("w_xq", 0, 256), 16, 256), (wsrc("w_xq", 256, 256), 16, 256),
                  (woxv[:, :, 0:1024], 4, 1024), (woxv[:, :, 1024:2048], 4, 1024)]
        make_ring(6, specsE)
        hxT, hxT_b = A.alloc("hxT", [16, 1024], BF16)
        memn, memn_b = A.alloc("memn", [16, 256], BF16)
        KxT, KxT_b = A.alloc("KxT", [4, 256], BF16)
        Vx, Vx_b = A.alloc("Vx", [2, 512], BF16)
        qx, qx_b = A.alloc("qx", [512], BF16)
        xo, xo_b = A.alloc("xo", [4, 512], BF16)
        Px = [A.alloc("Px%d" % i, [512], BF16) for i in range(2)]
        Rx, Rx_b = A.alloc("Rx", [512], F32)
        lnq, lnq_b = A.alloc("lnq2", [512], F32)
        sqa, sqa_b = A.alloc("sqa2", [512], BF16)
        A.push()
        mst, mst_b = A.alloc("mst", [16, 256], F32)
        dma("sp", mst, d["memT"].rearrange("(c p) t -> p c t", p=128), "dma9", writes=[mst_b])
        rmsnorm_T(mst, mst_b, 256, lambda c: nw[:, 3, c:c + 1], lambda c: memn[:, c, :], memn_b, 0)
        A.pop()
        wkxh = [wget(0), wget(1)]
        for hh in range(4):
            a, b2 = 2 * (hh % 2), 2 * (hh % 2) + 1
            wkx, wkx_b = wkxh[hh // 2]
            for kc in range(16):
                mm(ps[a][:, 0:256], wkx[:, kc, (hh % 2) * 128:(hh % 2) * 128 + 128], memn[:, kc, :], kc == 0, kc == 15,
                   [wkx_b, memn_b], [psb[a]])
            act(sqa[:, 0:256], ps[a][:, 0:256], AF.Square, [psb[a]], [sqa_b])
            mm(ps[b2][:, 0:256], ones_bf, sqa[:, 0:256], True, True, [cbf_b, sqa_b], [psb[b2]])
            rstd_from_ps(ps[b2][:, 0:256], psb[b2], lnq[:, 0:256], lnq_b, 1.0 / 128)
            stt(KxT[:, hh, :], ps[a][:, 0:256], sm[:, 4:5], lnq[:, 0:256], ALU.mult, ALU.mult,
                [psb[a], sm_b, lnq_b], [KxT_b])
        wrel(0); wrel(1)
        wvxh = [wget(2), wget(3)]
        for mb in range(2):
            for i in range(2):
                for kc in range(16):
                    mm(ps[4 + mb][:, 256 * i:256 * i + 256], memn[:, kc, mb * 128:(mb + 1) * 128], wvxh[i][0][:, kc, :],
                       kc == 0, kc == 15, [memn_b, wvxh[i][1]], [psb[4 + mb]])
            cp(Vx[:, mb, :], ps[4 + mb], [psb[4 + mb]], [Vx_b])
        for t in range(2):
            rmsnorm_T(h1[:, :, t * 512:(t + 1) * 512], h1_b, 512, lambda c: nw[:, 1, c:c + 1],
                      lambda c, t=t: hxT[:, c, t * 512:(t + 1) * 512], hxT_b, 6 + t)
        wrel(2); wrel(3)
        wqxh = [wget(4), wget(5)]
        woxh = [wget(6), wget(7)]
        for t in range(2):
            for hh in range(4):
                wqx, wqx_b = wqxh[hh // 2]
                for kc in range(16):
                    mm(ps[0], wqx[:, kc, (hh % 2) * 128:(hh % 2) * 128 + 128], hxT[:, kc, t * 512:(t + 1) * 512],
                       kc == 0, kc == 15, [wqx_b, hxT_b], [psb[0]])
                act(sqa, ps[0], AF.Square, [psb[0]], [sqa_b])
                mm(ps[1], ones_bf, sqa, True, True, [cbf_b, sqa_b], [psb[1]])
                rstd_from_ps(ps[1], psb[1], lnq, lnq_b, 1.0 / 128)
                stt(qx, ps[0], sm[:, 3:4], lnq, ALU.mult, ALU.mult, [psb[0], sm_b, lnq_b], [qx_b])
                for mb in range(2):
                    mm(ps[2 + mb], KxT[:, hh, mb * 128:(mb + 1) * 128], qx, True, True, [KxT_b, qx_b], [psb[2 + mb]])
                    act(Px[mb][0], ps[2 + mb], AF.Exp, [psb[2 + mb]], [Px[mb][1]])
                for mb in range(2):
                    mm(ps[4], Vx[:, mb, hh * 128:(hh + 1) * 128], Px[mb][0], mb == 0, mb == 1,
                       [Vx_b, Px[mb][1]], [psb[4]])
                    mm(ps[5], ones_bf, Px[mb][0], mb == 0, mb == 1, [cbf_b, Px[mb][1]], [psb[5]])
                act(Rx, ps[5], AF.Ln, [psb[5]], [Rx_b])
                act(Rx, Rx, AF.Exp, [Rx_b], [Rx_b], scale=-1.0)
                tt(xo[:, hh, :], ps[4], Rx, ALU.mult, [psb[4], Rx_b], [xo_b])
            for o in range(16):
                bank = 6 + o % 2
                wox, wox_b = woxh[o // 8]
                for hh in range(4):
                    mm(ps[bank], wox[:, hh, (o % 8) * 128:(o % 8) * 128 + 128], xo[:, hh, :], hh == 0, hh == 3,
                       [wox_b, xo_b], [psb[bank]])
                tt(h1[:, o, t * 512:(t + 1) * 512], h1[:, o, t * 512:(t + 1) * 512], ps[bank], ALU.add,
                   [h1_b, psb[bank]], [h1_b])
        A.pop()
        P.barrier()
        if dbg == "h2":
            dump(h1[:, 5, :], h1_b, 1024)
            finish(); return nc

        A.push()
        specsF = []
        for half_ in range(2):
            f0_ = half_ * 2816
            for fb_ in range(11):
                specsF += [(wsrc("w_g", f0_ + fb_ * 256, 256), 16, 256), (wsrc("w_u", f0_ + fb_ * 256, 256), 16, 256)]
            wdv_ = d["w_d"][f0_:f0_ + 2816, :].rearrange("(k p) c -> p k c", p=128)
            specsF += [(wdv_[:, :, o_ * 128:(o_ + 1) * 128], 22, 128) for o_ in range(16)]
        make_ring(5, specsF)
        hfT, hfT_b = A.alloc("hfT", [16, 1024], BF16)
        actT, actT_b = A.alloc("actT", [22, 1024], BF16)
        sg = [A.alloc("sg%d" % i, [512], F32) for i in range(2)]
        for t in range(2):
            rmsnorm_T(h1[:, :, t * 512:(t + 1) * 512], h1_b, 512, lambda c: nw[:, 2, c:c + 1],
                      lambda c, t=t: hfT[:, c, t * 512:(t + 1) * 512], hfT_b, t)
        si = 0
        for half in range(2):
            f0 = half * 2816
            for fb in range(11):
                fidx = half * 38 + 2 * fb
                wg, wg_b = wget(fidx)
                wu, wu_b = wget(fidx + 1)
                for j in range(2):
                    cc = fb * 2 + j
                    for t in range(2):
                        bg, bu = (0, 1) if (si % 2 == 0) else (2, 3)
                        for kc in range(16):
                            mm(ps[bg], wg[:, kc, j * 128:(j + 1) * 128], hfT[:, kc, t * 512:(t + 1) * 512],
                               kc == 0, kc == 15, [wg_b, hfT_b], [psb[bg]])
                        for kc in range(16):
                            mm(ps[bu], wu[:, kc, j * 128:(j + 1) * 128], hfT[:, kc, t * 512:(t + 1) * 512],
                               kc == 0, kc == 15, [wu_b, hfT_b], [psb[bu]])
                        s_ap, s_b = sg[si % 2]
                        si += 1
                        act(s_ap, ps[bg], AF.Silu, [psb[bg]], [s_b])
                        tt(actT[:, cc, t * 512:(t + 1) * 512], s_ap, ps[bu], ALU.mult, [s_b, psb[bu]], [actT_b])
                wrel(fidx); wrel(fidx + 1)
            for o in range(16):
                wd, wd_b = wget(half * 38 + 22 + o)
                for t in range(2):
                    bank = 4 + (2 * o + t) % 4
                    for kc in range(22):
                        mm(ps[bank], wd[:, kc, :], actT[:, kc, t * 512:(t + 1) * 512], kc == 0, kc == 21,
                           [wd_b, actT_b], [psb[bank]])
                    tt(h1[:, o, t * 512:(t + 1) * 512], h1[:, o, t * 512:(t + 1) * 512], ps[bank], ALU.add,
                       [h1_b, psb[bank]], [h1_b])
                wrel(half * 38 + 22 + o)
        outv = outT.rearrange("(c p) t -> p c t", p=128)
        for q in range(4):
            dma("sp", outv[:, 4 * q:4 * q + 4, :], h1[:, 4 * q:4 * q + 4, :], "dma%d" % (10 + q), reads=[h1_b])
        A.pop()
        finish()
        print("[kernel] ops=%d arena_peak_words=%d of %d" % (P.n_ops, A.peak, A.words))
    return nc


_BF = ml_dtypes.bfloat16


def _consts():
    ident = np.eye(128, dtype=np.float32)
    ones = np.ones((128, 128), np.float32)
    blk = np.zeros((128, 128), np.float32)
    blk[:64, :64] = 1.0
    blk[64:, 64:] = 1.0
    triU = np.triu(np.ones((128, 128), np.float32))
    SL = np.tril(np.ones((128, 128), np.float32), -1)
    c_bf = np.stack([ident, ones, blk, triU, SL], axis=1).astype(_BF)
    c_f32 = np.ascontiguousarray(np.stack([triU, SL, ones], axis=1))
    return np.ascontiguousarray(c_bf), c_f32


def _chunked(v):
    return np.ascontiguousarray(np.asarray(v, np.float32).reshape(-1, 128).T)


def make_in_maps(inp):
    f = lambda k: np.asarray(inp[k], np.float32)
    x = f("x"); mem = f("mem")
    c_bf, c_f32 = _consts()
    nw = np.stack([_chunked(f("mix_norm_w")[0]), _chunked(f("xattn_norm_w")[0]), _chunked(f("ffn_norm_w")[0]),
                   _chunked(f("mem_norm_w")[0]), _chunked(f("ssd_norm_w")[0])], axis=1)
    pp = np.zeros((128, 8), np.float32)
    pp[:, 0] = np.tile(f("da_q_norm_w")[0], 2)
    pp[:, 1] = np.tile(f("da_k_norm_w")[0], 2)
    pp[:, 2] = f("da_subln_w")[0]
    pp[:, 3] = f("xattn_q_norm_w")[0]
    pp[:, 4] = f("xattn_k_norm_w")[0]
    rows = np.concatenate([f("da_lambda_q1")[0], f("da_lambda_k1")[0], f("da_lambda_q2")[0], f("da_lambda_k2")[0],
                           f("ssd_dt_bias")[0], f("ssd_a_log")[0], f("ssd_d")[0]])[None, :]
    cw = f("ssd_conv_w")[0]
    convw = np.ascontiguousarray(cw.reshape(4, 24, 128).transpose(2, 1, 0))
    convb = _chunked(f("ssd_conv_b")[0])
    shared = {
        "w_in": np.ascontiguousarray(f("w_in")[0]), "w_out": np.ascontiguousarray(f("w_out")[0]),
        "w_xq": np.ascontiguousarray(f("xattn_w_q")[0]), "w_xkv": np.ascontiguousarray(f("xattn_w_kv")[0]),
        "w_xo": np.ascontiguousarray(f("xattn_w_o")[0]),
        "w_g": np.ascontiguousarray(f("ffn_w_gate")[0]), "w_u": np.ascontiguousarray(f("ffn_w_up")[0]),
        "w_d": np.ascontiguousarray(f("ffn_w_down")[0]),
        "nw": np.ascontiguousarray(nw), "pp": pp, "rows": np.ascontiguousarray(rows.astype(np.float32)),
        "convw": convw, "convb": convb, "convb_row": np.ascontiguousarray(f("ssd_conv_b")[0][None, :]),
        "c_bf": c_bf, "c_f32": c_f32,
    }
    maps = []
    for core in range(8):
        b, half = core // 2, core % 2
        xT = np.zeros((2048, 2048), np.float32)
        if half == 0:
            xT[:, 1024:] = x[b, 0:1024].T
        else:
            xT[:, :1024] = x[b, 0:1024].T
            xT[:, 1024:] = x[b, 1024:2048].T
        m = dict(shared)
        m["xT"] = xT
        m["memT"] = np.ascontiguousarray(mem[b].T)
        m["flag"] = np.full((128, 1), float(half), np.float32)
        maps.append(m)
    return maps


_NC_CACHE = {}


def kernel(**inputs):
    if "nc" not in _NC_CACHE:
        _NC_CACHE["nc"] = build_program()
    nc = _NC_CACHE["nc"]
    maps = make_in_maps(inputs)
    res = run_bass_kernel_spmd(nc, maps, core_ids=list(range(8)))
    out = np.empty((4, 2048, 2048), np.float32)
    for core in range(8):
        b, half = core // 2, core % 2
        out[b, half * 1024:(half + 1) * 1024, :] = res.results[core]["outT"].T
    return out
```
